# Optimizing a Trainium2 kernel written in Bass

```python
import jax
import jax.numpy as jnp
from jax import lax
import numpy as np

D_MODEL = 1024
BATCH = 8
SEQ = 4096
DEPTH = 2

GRID_W = 64
CTX_LEN = 256
EPS = 1e-6

FOURIER_GROUPS = 4
FOURIER_DIM = D_MODEL // 2
FOURIER_GROUP_DIM = FOURIER_DIM // FOURIER_GROUPS

SSD_D_INNER = D_MODEL
SSD_HEAD_DIM = 64
SSD_HEADS = SSD_D_INNER // SSD_HEAD_DIM
SSD_GROUPS = 2
SSD_STATE = 128
SSD_GN = SSD_GROUPS * SSD_STATE
SSD_CONV = 5
SSD_CONV_DIM = SSD_D_INNER + 2 * SSD_GN
SSD_CHUNK = 128

EVEN_IN = FOURIER_DIM + SSD_D_INNER + SSD_CONV_DIM + 2 * SSD_HEADS
EVEN_MIX = FOURIER_DIM + SSD_D_INNER

SGU_DIM = D_MODEL
SGU_GROUPS = 4
SGU_GROUP_DIM = SGU_DIM // SGU_GROUPS
CHUNK_ROWS = 2
SGU_CHUNK = CHUNK_ROWS * GRID_W

POOL_WINDOWS = (2, 4, 8, 16)
POOL_GROUPS = len(POOL_WINDOWS)
POOL_DIM = D_MODEL // 2
POOL_GROUP_DIM = POOL_DIM // POOL_GROUPS

ODD_IN = 2 * SGU_DIM + POOL_DIM
ODD_MIX = SGU_DIM + POOL_DIM

D_FF = 4 * D_MODEL

N_EVEN = (DEPTH + 1) // 2
N_ODD = DEPTH // 2

kernel_name = 'hybrid_fourier_ssd_sgu_pool_dit'


def rms_norm(x, g):
    xf = x.astype(jnp.float32)
    y = xf * lax.rsqrt(jnp.mean(xf * xf, axis=-1, keepdims=True) + EPS)
    return (y * g.astype(jnp.float32)).astype(x.dtype)


def layer_norm(x, g, b):
    xf = x.astype(jnp.float32)
    xc = xf - jnp.mean(xf, axis=-1, keepdims=True)
    y = xc * lax.rsqrt(jnp.mean(xc * xc, axis=-1, keepdims=True) + EPS)
    return (y * g.astype(jnp.float32) + b.astype(jnp.float32)).astype(x.dtype)


def adaln(cond, w, b, n):
    m = (jax.nn.silu(cond) @ w + b)[..., None, :]
    return jnp.split(m, n, axis=-1)


def modulate(x, g, shift, scale):
    return rms_norm(x, g) * (1 + scale) + shift


def squared_relu_mlp(h, w1, w2):
    return jnp.square(jax.nn.relu(h @ w1)) @ w2


def context_needed_after(layer):
    return any(j % 2 == 0 for j in range(layer + 1, DEPTH))


def fourier_mix(u):
    b_, l, _ = u.shape
    ug = u.astype(jnp.float32).reshape(b_, l, FOURIER_GROUPS, FOURIER_GROUP_DIM)
    y = jnp.fft.fft2(ug, axes=(1, 3), norm='ortho').real
    return y.reshape(b_, l, FOURIER_DIM).astype(u.dtype)


def depthwise_conv_centred(x, w, b):
    pad = w.shape[0] // 2
    y = lax.conv_general_dilated(x, w[:, None, :], window_strides=(1,), padding=[(pad, pad)],
                                 dimension_numbers=('NWC', 'WIO', 'NWC'),
                                 feature_group_count=x.shape[-1])
    return y + b


def zero_states(b_):
    z = jnp.zeros((b_, SSD_HEADS, SSD_HEAD_DIM, SSD_STATE), jnp.float32)
    return (z, z)


def ssd_inputs(xbc_dt, conv_w, conv_b, dt_bias):
    xbc = jax.nn.silu(depthwise_conv_centred(xbc_dt[..., :SSD_CONV_DIM], conv_w, conv_b))
    b_, l, _ = xbc.shape
    xs = xbc[..., :SSD_D_INNER].reshape(b_, l, SSD_HEADS, SSD_HEAD_DIM)
    bm = xbc[..., SSD_D_INNER:SSD_D_INNER + SSD_GN].reshape(b_, l, SSD_GROUPS, SSD_STATE)
    cm = xbc[..., SSD_D_INNER + SSD_GN:].reshape(b_, l, SSD_GROUPS, SSD_STATE)
    dt_raw = xbc_dt[..., SSD_CONV_DIM:].astype(jnp.float32).reshape(b_, l, 2, SSD_HEADS)
    dt = jax.nn.softplus(dt_raw + dt_bias.astype(jnp.float32))
    return xs, bm, cm, dt


def ssd_scan(xs, dt, a, bm, cm, h0, return_y):
    b_, l, nh, hp = xs.shape
    ng, ns = bm.shape[2], bm.shape[3]
    hg = nh // ng
    q = SSD_CHUNK
    nc = l // q
    f32 = jnp.float32
    x = xs.astype(f32).reshape(b_, nc, q, ng, hg, hp)
    dtc = dt.reshape(b_, nc, q, ng, hg)
    bc = bm.astype(f32).reshape(b_, nc, q, ng, ns)
    cum = jnp.cumsum(dtc * a.astype(f32).reshape(ng, hg), axis=2)
    xdt = x * dtc[..., None]
    to_end = jnp.exp(cum[:, :, -1:] - cum)
    states = jnp.einsum('bcjgn,bcjgh,bcjghp->bcghpn', bc, to_end, xdt)
    chunk_decay = jnp.exp(cum[:, :, -1])

    def step(h, inp):
        st, dec = inp
        return h * dec[..., None, None] + st, (h if return_y else None)

    h_last, h_prev = lax.scan(step, h0.reshape(b_, ng, hg, hp, ns),
                              (jnp.moveaxis(states, 1, 0), jnp.moveaxis(chunk_decay, 1, 0)))
    h_last = h_last.reshape(b_, nh, hp, ns)
    if not return_y:
        return None, h_last
    cc = cm.astype(f32).reshape(b_, nc, q, ng, ns)
    seg = cum[:, :, :, None] - cum[:, :, None, :]
    lower = jnp.tril(jnp.ones((q, q), dtype=bool))[None, None, :, :, None, None]
    decay = jnp.exp(jnp.where(lower, seg, -jnp.inf))
    cb = jnp.einsum('bcign,bcjgn->bcijg', cc, bc)
    y_diag = jnp.einsum('bcijgh,bcjghp->bcighp', cb[..., None] * decay, xdt)
    h_prev = jnp.moveaxis(h_prev, 0, 1)
    y_off = jnp.einsum('bcign,bcghpn->bcighp', cc, h_prev) * jnp.exp(cum)[..., None]
    return (y_diag + y_off).reshape(b_, l, nh, hp), h_last


def ssd_bidirectional(xs, bm, cm, dt, a_log, h0, return_y):
    a = -jnp.exp(a_log.astype(jnp.float32))
    rev = lambda t: jnp.flip(t, axis=1)
    y_f, h_f = ssd_scan(xs, dt[:, :, 0], a[0], bm, cm, h0[0], return_y)
    y_b, h_b = ssd_scan(rev(xs), rev(dt[:, :, 1]), a[1], rev(bm), rev(cm), h0[1], return_y)
    y = (y_f + rev(y_b)) if return_y else None
    return y, (h_f, h_b)


def group_rms_norm(y, g):
    b_, l, d = y.shape
    yg = y.reshape(b_, l, SSD_GROUPS, d // SSD_GROUPS)
    yg = yg * lax.rsqrt(jnp.mean(yg * yg, axis=-1, keepdims=True) + EPS)
    return yg.reshape(b_, l, d) * g.astype(jnp.float32)


def even_mixer(h, w_in, conv_w, conv_b, dt_bias, a_log, d_skip, ssd_norm_g, w_out, h0):
    proj = h @ w_in
    f_in = proj[..., :FOURIER_DIM]
    z = proj[..., FOURIER_DIM:FOURIER_DIM + SSD_D_INNER]
    xs, bm, cm, dt = ssd_inputs(proj[..., FOURIER_DIM + SSD_D_INNER:], conv_w, conv_b, dt_bias)
    y, states = ssd_bidirectional(xs, bm, cm, dt, a_log, h0, True)
    y = y + d_skip.astype(jnp.float32)[:, None] * xs.astype(jnp.float32)
    b_, l = y.shape[0], y.shape[1]
    y = y.reshape(b_, l, SSD_D_INNER) * jax.nn.silu(z.astype(jnp.float32))
    y = group_rms_norm(y, ssd_norm_g).astype(h.dtype)
    out = jnp.concatenate([fourier_mix(f_in), y], axis=-1) @ w_out
    return out, states


def context_scan_states(hc, w_in, conv_w, conv_b, dt_bias, a_log):
    proj = hc @ w_in[:, FOURIER_DIM + SSD_D_INNER:]
    xs, bm, cm, dt = ssd_inputs(proj, conv_w, conv_b, dt_bias)
    _, states = ssd_bidirectional(xs, bm, cm, dt, a_log, zero_states(hc.shape[0]), False)
    return states


def multiscale_pool(p):
    b_, l, _ = p.shape
    pf = p.astype(jnp.float32).reshape(b_, l, POOL_GROUPS, POOL_GROUP_DIM)
    cs = jnp.concatenate([jnp.zeros_like(pf[:, :1]), jnp.cumsum(pf, axis=1)], axis=1)
    pos = jnp.arange(l)
    outs = []
    for gi, w in enumerate(POOL_WINDOWS):
        r = w // 2
        lo = jnp.maximum(pos - r, 0)
        hi = jnp.minimum(pos + r + 1, l)
        csg = cs[:, :, gi]
        mean = (csg[:, hi] - csg[:, lo]) / (hi - lo).astype(jnp.float32)[None, :, None]
        outs.append(mean - pf[:, :, gi])
    return jnp.stack(outs, axis=2).astype(p.dtype)


def odd_mixer(h, w_in, sgu_ln_g, sgu_ln_b, w_s, b_s, pool_w, pool_scale, w_out, n_chunks):
    proj = h @ w_in
    uv = jax.nn.gelu(proj[..., :2 * SGU_DIM])
    u, v = uv[..., :SGU_DIM], uv[..., SGU_DIM:]
    v = layer_norm(v, sgu_ln_g, sgu_ln_b)
    b_, l, _ = v.shape
    vc = v.reshape(b_, n_chunks, SGU_CHUNK, SGU_GROUPS, SGU_GROUP_DIM)
    v_mix = jnp.einsum('gij,bcjgd->bcigd', w_s, vc) + jnp.swapaxes(b_s, 0, 1)[:, :, None]
    sgu = u * v_mix.reshape(b_, l, SGU_DIM)
    pooled = multiscale_pool(proj[..., 2 * SGU_DIM:])
    pool_out = jnp.einsum('blgd,gde->blge', pooled, pool_w).reshape(b_, l, POOL_DIM) * pool_scale
    return jnp.concatenate([sgu, pool_out], axis=-1) @ w_out


def setup_inputs(seed: int = 0) -> dict:
    key = jax.random.key(seed)
    keys = iter(jax.random.split(key, 32))
    f32 = jnp.float32

    def nrm(shape, scale):
        return jax.random.normal(next(keys), shape, f32) * scale

    def gain(shape):
        return 1.0 + nrm(shape, 0.02)

    dt0 = jnp.exp(jax.random.uniform(next(keys), (N_EVEN, 2, SSD_HEADS), f32,
                                     float(np.log(1e-3)), float(np.log(1e-1))))
    return {
        'x': nrm((BATCH, SEQ, D_MODEL), 1.0),
        'c': nrm((BATCH, D_MODEL), 1.0),
        'ctx': nrm((BATCH, CTX_LEN, D_MODEL), 1.0),
        'c_ctx': nrm((D_MODEL,), 1.0),
        'ada_w': nrm((DEPTH, D_MODEL, 6 * D_MODEL), 0.5 * D_MODEL ** -0.5),
        'ada_b': nrm((DEPTH, 6 * D_MODEL), 0.02),
        'mix_norm_g': gain((DEPTH, D_MODEL)),
        'ffn_norm_g': gain((DEPTH, D_MODEL)),
        'ffn_w1': nrm((DEPTH, D_MODEL, D_FF), D_MODEL ** -0.5),
        'ffn_w2': nrm((DEPTH, D_FF, D_MODEL), D_FF ** -0.5),
        'ev_w_in': nrm((N_EVEN, D_MODEL, EVEN_IN), D_MODEL ** -0.5),
        'ev_conv_w': nrm((N_EVEN, SSD_CONV, SSD_CONV_DIM), SSD_CONV ** -0.5),
        'ev_conv_b': nrm((N_EVEN, SSD_CONV_DIM), 0.02),
        'ev_dt_bias': dt0 + jnp.log(-jnp.expm1(-dt0)),
        'ev_a_log': jnp.log(jax.random.uniform(next(keys), (N_EVEN, 2, SSD_HEADS), f32, 1.0, 16.0)),
        'ev_d_skip': gain((N_EVEN, SSD_HEADS)),
        'ev_ssd_norm_g': gain((N_EVEN, SSD_D_INNER)),
        'ev_w_out': nrm((N_EVEN, EVEN_MIX, D_MODEL), EVEN_MIX ** -0.5),
        'od_w_in': nrm((N_ODD, D_MODEL, ODD_IN), D_MODEL ** -0.5),
        'od_sgu_ln_g': gain((N_ODD, SGU_DIM)),
        'od_sgu_ln_b': nrm((N_ODD, SGU_DIM), 0.02),
        'od_w_s': nrm((N_ODD, SGU_GROUPS, SGU_CHUNK, SGU_CHUNK), SGU_CHUNK ** -0.5),
        'od_b_s': gain((N_ODD, SGU_GROUPS, SGU_CHUNK)),
        'od_pool_w': nrm((N_ODD, POOL_GROUPS, POOL_GROUP_DIM, POOL_GROUP_DIM), POOL_GROUP_DIM ** -0.5),
        'od_pool_scale': gain((N_ODD, POOL_DIM)),
        'od_w_out': nrm((N_ODD, ODD_MIX, D_MODEL), ODD_MIX ** -0.5),
        'final_g': gain((D_MODEL,)),
    }


def reference(x, c, ctx, c_ctx, ada_w, ada_b, mix_norm_g, ffn_norm_g, ffn_w1, ffn_w2,
              ev_w_in, ev_conv_w, ev_conv_b, ev_dt_bias, ev_a_log, ev_d_skip, ev_ssd_norm_g, ev_w_out,
              od_w_in, od_sgu_ln_g, od_sgu_ln_b, od_w_s, od_b_s, od_pool_w, od_pool_scale, od_w_out,
              final_g):
    rows = x.shape[1] // GRID_W
    n_lat_chunks = rows // CHUNK_ROWS
    n_ctx_chunks = ctx.shape[1] // SGU_CHUNK
    lat, cur = x, ctx
    for layer in range(DEPTH):
        ctx_out = context_needed_after(layer)
        s_m, sc_m, g_m, s_f, sc_f, g_f = adaln(c, ada_w[layer], ada_b[layer], 6)
        h = modulate(lat, mix_norm_g[layer], s_m, sc_m)
        if ctx_out:
            cmod = adaln(c_ctx, ada_w[layer], ada_b[layer], 6)
            hc = modulate(cur, mix_norm_g[layer], cmod[0], cmod[1])
        elif layer % 2 == 0:
            cmod = adaln(c_ctx, ada_w[layer][:, :2 * D_MODEL], ada_b[layer][:2 * D_MODEL], 2)
            hc = modulate(cur, mix_norm_g[layer], cmod[0], cmod[1])
        if layer % 2 == 0:
            e = layer // 2
            even_p = (ev_w_in[e], ev_conv_w[e], ev_conv_b[e], ev_dt_bias[e], ev_a_log[e],
                      ev_d_skip[e], ev_ssd_norm_g[e], ev_w_out[e])
            if ctx_out:
                ctx_mix, states = even_mixer(hc, *even_p, zero_states(cur.shape[0]))
            else:
                states = context_scan_states(hc, *even_p[:5])
            lat_mix, _ = even_mixer(h, *even_p, states)
        else:
            o = layer // 2
            odd_p = (od_w_in[o], od_sgu_ln_g[o], od_sgu_ln_b[o], od_w_s[o], od_b_s[o],
                     od_pool_w[o], od_pool_scale[o], od_w_out[o])
            lat_mix = odd_mixer(h, *odd_p, n_lat_chunks)
            if ctx_out:
                ctx_mix = odd_mixer(hc, *odd_p, n_ctx_chunks)
        lat = lat + g_m * lat_mix
        lat = lat + g_f * squared_relu_mlp(modulate(lat, ffn_norm_g[layer], s_f, sc_f),
                                           ffn_w1[layer], ffn_w2[layer])
        if ctx_out:
            cur = cur + cmod[2] * ctx_mix
            cur = cur + cmod[5] * squared_relu_mlp(modulate(cur, ffn_norm_g[layer], cmod[3], cmod[4]),
                                                   ffn_w1[layer], ffn_w2[layer])
    return rms_norm(lat, final_g)
```

```python
import contextlib
import numpy as np
import ml_dtypes
import concourse.bass as bass
import concourse.mybir as mybir
from concourse.bass_utils import run_bass_kernel_spmd

F32 = mybir.dt.float32
BF16 = mybir.dt.bfloat16
AF = mybir.ActivationFunctionType
ALU = mybir.AluOpType

L = 4096
DM = 1024
CTXL = 256
EPS = 1e-6
DEBUG = False


class Prog:
    COMPUTE = ('pe', 'act', 'dve', 'pool')

    def __init__(self, nc, ndma_sems=8):
        self.nc = nc
        self.ops = []
        self.last_w = {}
        self.readers = {}
        self.ndma = ndma_sems
        self.grp_n = {'cast': 13}

    def add(self, eng, fn, r=(), w=(), dma=False, grp=None):
        idx = len(self.ops)
        deps = set()
        for k in r:
            if k in self.last_w:
                deps.add(self.last_w[k])
            if isinstance(k, tuple) and k[0] == 'ps':
                for rd in self.readers.get(k, ()):
                    if self.ops[rd][0] != eng:
                        deps.add(rd)
        for k in w:
            if k in self.last_w:
                deps.add(self.last_w[k])
            for rd in self.readers.get(k, ()):
                deps.add(rd)
        self.ops.append([eng, fn, deps, dma, (grp or eng) if dma else None])
        for k in w:
            self.last_w[k] = idx
            self.readers[k] = []
        for k in r:
            lst = self.readers.setdefault(k, [])
            if not dma:
                for j in range(len(lst) - 1, -1, -1):
                    o = self.ops[lst[j]]
                    if (not o[3]) and o[0] == eng:
                        del lst[j]
            lst.append(idx)
        return idx

    def barrier(self):
        n = len(self.ops)
        deps = set()
        seen_c = set()
        dcount = {}
        for i in range(n - 1, max(-1, n - 20000), -1):
            eng, fn, d, dma, grp = self.ops[i]
            if dma:
                if grp != 'cast' and dcount.get(grp, 0) < self.ndma:
                    deps.add(i)
                    dcount[grp] = dcount.get(grp, 0) + 1
            elif eng not in seen_c:
                seen_c.add(eng)
                deps.add(i)
        b0 = len(self.ops)
        self.ops.append(['act', lambda e: e.nop(), deps, False, None])
        for eng in ('pe', 'dve', 'pool', 'sp', 'act'):
            self.ops.append([eng, lambda e: e.nop(), {b0}, False, None])
        keep = {k: v for k, v in self.last_w.items() if self.ops[v][4] == 'cast'}
        self.last_w = keep
        self.readers = {}

    def emit(self):
        nc = self.nc
        ops = self.ops
        n = len(ops)
        engs = ['pe', 'act', 'dve', 'pool', 'sp']
        need = [False] * n
        for i, (eng, fn, deps, dma, grp) in enumerate(ops):
            for d in deps:
                de, _, _, ddma, _g = ops[d]
                if (not ddma) and de == 'pe' and eng == 'pe' and not dma:
                    continue
                need[d] = True
        val = [0] * n
        semof = [None] * n
        cnt = {e: 0 for e in engs}
        dcnt = {}
        slotcnt = {}
        prev_on_slot = [None] * n
        last_on_slot = {}
        for i, (eng, fn, deps, dma, grp) in enumerate(ops):
            if dma:
                ns = self.grp_n.get(grp, self.ndma)
                s = dcnt.get(grp, 0) % ns
                dcnt[grp] = dcnt.get(grp, 0) + 1
                key = (grp, s)
                slotcnt[key] = slotcnt.get(key, 0) + 16
                val[i] = slotcnt[key]
                semof[i] = ('d', grp, s)
                prev_on_slot[i] = last_on_slot.get(key)
                last_on_slot[key] = i
            elif need[i]:
                cnt[eng] += 1
                val[i] = cnt[eng]
                semof[i] = ('c', eng, 0)
        self.stats = dict(n=n, incs=dict(cnt), dmas=dict(dcnt))
        with contextlib.ExitStack() as es:
            csem = {e: es.enter_context(nc.semaphore(f"c_{e}")) for e in engs}
            dsem = {}
            for g in dcnt:
                for s in range(self.grp_n.get(g, self.ndma)):
                    dsem[(g, s)] = es.enter_context(nc.semaphore(f"d_{g}{s}"))
            block = es.enter_context(nc.Block())
            per_eng = {e: [i for i in range(n) if ops[i][0] == e] for e in engs}
            nwaits = [0]

            def semh(so):
                return csem[so[1]] if so[0] == 'c' else dsem[(so[1], so[2])]

            def body(e, eh):
                known = {}
                for i in per_eng[e]:
                    eng, fn, deps, dma, grp = ops[i]
                    req = {}
                    dl = list(deps)
                    if dma and prev_on_slot[i] is not None:
                        dl.append(prev_on_slot[i])
                    for d in dl:
                        de, _, _, ddma, _g = ops[d]
                        if (not ddma) and de == 'pe' and eng == 'pe' and not dma:
                            continue
                        so = semof[d]
                        if val[d] > req.get(so, 0):
                            req[so] = val[d]
                    for so, v in req.items():
                        if known.get(so, 0) >= v:
                            continue
                        eh.wait_ge(semh(so), v)
                        known[so] = v
                        nwaits[0] += 1
                    ins = fn(eh)
                    if dma:
                        ins.then_inc(semh(semof[i]), 16)
                    elif need[i]:
                        ins.then_inc(semh(semof[i]), 1)

            if per_eng['pe']:
                @block.tensor
                def _(eh):
                    body('pe', eh)
            if per_eng['act']:
                @block.scalar
                def _(eh):
                    body('act', eh)
            if per_eng['dve']:
                @block.vector
                def _(eh):
                    body('dve', eh)
            if per_eng['pool']:
                @block.gpsimd
                def _(eh):
                    body('pool', eh)
            if per_eng['sp']:
                @block.sync
                def _(eh):
                    body('sp', eh)
            self.stats['waits'] = nwaits[0]


COLS = {}
_c = 0
for _name, _n in [('gmix0', 8), ('gmix1', 8), ('gffn0', 8), ('gffn1', 8), ('adab0', 48), ('adab1', 48),
                  ('convw', 60), ('convb', 12), ('ssdg', 8), ('pscale', 4), ('fing', 8), ('cb', 8), ('cctx', 8),
                  ('lnb', 8)]:
    COLS[_name] = (_c, _n)
    _c += _n
NCOLS = _c
ROWS = {}
_c = 0
for _name, _n in [('dtbias', 32), ('alog', 32), ('dskip', 16), ('bs', 512), ('lng', 1024), ('convb', 1536)]:
    ROWS[_name] = (_c, _n)
    _c += _n
NROWS = _c


class Builder:
    def __init__(self):
        self.nc = bass.Bass("TRN2", target_bir_lowering=False)
        self.P = Prog(self.nc)
        self.bank_ctr = 0
        self.es = contextlib.ExitStack()
        self.uid = 0
        self.ldq = 'pool'
        self.stq = 'sp'

    def mm(self, out, lhsT, rhs, start, stop, r, w):
        self.P.add('pe', lambda e: e.matmul(out, lhsT=lhsT, rhs=rhs, start=start, stop=stop), r=r, w=w)

    def tr(self, out, in_, ident, r, w):
        self.P.add('pe', lambda e: e.transpose(out=out, in_=in_, identity=ident), r=r, w=w)

    def act(self, out, in_, func, r, w, scale=None, bias=None, accum_out=None):
        kw = {}
        if scale is not None:
            kw['scale'] = scale
        if bias is not None:
            kw['bias'] = bias
        if accum_out is not None:
            kw['accum_out'] = accum_out
        self.P.add('act', lambda e: e.activation(out=out, in_=in_, func=func, **kw), r=r, w=w)

    def tt(self, eng, out, in0, in1, op, r, w):
        self.P.add(eng, lambda e: e.tensor_tensor(out=out, in0=in0, in1=in1, op=op), r=r, w=w)

    def ts(self, eng, out, in0, s1, s2, op0, op1, r, w):
        if op1 is None:
            self.P.add(eng, lambda e: e.tensor_scalar(out=out, in0=in0, scalar1=s1, scalar2=None, op0=op0), r=r, w=w)
        else:
            self.P.add(eng, lambda e: e.tensor_scalar(out=out, in0=in0, scalar1=s1, scalar2=s2, op0=op0, op1=op1), r=r, w=w)

    def stt(self, out, in0, scalar, in1, op0, op1, r, w):
        self.P.add('dve', lambda e: e.scalar_tensor_tensor(out=out, in0=in0, scalar=scalar, in1=in1, op0=op0, op1=op1), r=r, w=w)

    def copy(self, eng, out, in_, r, w):
        if eng == 'act':
            self.P.add('act', lambda e: e.activation(out=out, in_=in_, func=AF.Copy), r=r, w=w)
        else:
            self.P.add(eng, lambda e: e.tensor_copy(out=out, in_=in_), r=r, w=w)

    def memset(self, eng, ap, val, w):
        self.P.add(eng, lambda e: e.memset(ap, val), w=w)

    def dma(self, q, out, in_, r, w, nonc=False, grp=None):
        nc = self.nc
        if nonc:
            def f(e):
                with nc.allow_non_contiguous_dma(reason="small strided"):
                    return e.dma_start(out=out, in_=in_)
        else:
            def f(e):
                return e.dma_start(out=out, in_=in_)
        self.P.add(q, f, r=r, w=w, dma=True, grp=grp)

    def ld(self, out, in_, r, w, nonc=False):
        self.dma(self.ldq, out, in_, r, w, nonc=nonc)

    def st(self, out, in_, r, w, nonc=False):
        self.dma(self.stq, out, in_, r, w, nonc=nonc)

    def nb(self):
        b = self.bank_ctr % 8
        self.bank_ctr += 1
        return b

    def nb2(self):
        if self.bank_ctr % 2:
            self.bank_ctr += 1
        b = self.bank_ctr % 8
        self.bank_ctr += 2
        return b

    def key(self, s):
        self.uid += 1
        return (s, self.uid)

    def din(self, name, shape, dt=F32):
        return self.nc.dram_tensor(name, shape, dt, kind="ExternalInput").ap()

    def dscr(self, name, shape, dt):
        return self.nc.dram_tensor(name, shape, dt, kind="Internal").ap()

    def sb(self, name, shape, dt):
        return self.es.enter_context(self.nc.sbuf_tensor("sb_" + name, shape, dt))

    def arena_reset(self):
        self.aoff = 0

    def carve(self, shape, dt):
        n = 1
        for s in shape:
            n *= s
        nbytes = n * (4 if dt == F32 else 2)
        nbytes = (nbytes + 63) // 64 * 64
        off = self.aoff
        self.aoff += nbytes
        assert self.aoff <= self.arena_bytes, (self.aoff, self.arena_bytes)
        a = self.arena[:, off // 2: off // 2 + (n * (2 if dt == F32 else 1))]
        if dt == F32:
            a = a.bitcast(F32)
        if len(shape) == 1:
            return a
        if len(shape) == 2:
            return a.rearrange("p (a b) -> p a b", a=shape[0])
        if len(shape) == 3:
            return a.rearrange("p (a b c) -> p a b c", a=shape[0], b=shape[1])
        raise ValueError

    def build(self):
        nc = self.nc
        B = self
        x = B.din("x", [L, DM])
        ctx = B.din("ctx", [CTXL, DM])
        cols_d = B.din("cols", [128, NCOLS])
        rows_d = B.din("rows", [1, NROWS])
        ada_w = B.din("ada_w", [2, 128, 8 * 6144])
        w_in0 = B.din("w_in0", [128, 8 * 3104])
        w_out0 = B.din("w_out0", [128, 12 * 1024])
        w1 = B.din("w1", [2, 128, 8 * 4096])
        w2 = B.din("w2", [2, 128, 32 * 1024])
        w_in1 = B.din("w_in1", [128, 8 * 2560])
        w_out1 = B.din("w_out1", [128, 12 * 1024])
        w_sT = B.din("w_sT", [128, 4 * 128])
        pool_w = B.din("pool_w", [128, 4 * 128])
        ident_d = B.din("ident", [128, 128])
        cb16_d = B.din("cb16", [128, 256 + 4 * 128 + 128 + 128 + 20 * 128], BF16)
        tcs_d = B.din("tcs", [32, 128, 2 * 4096], BF16)
        out = nc.dram_tensor("out", [L, DM], F32, kind="ExternalOutput").ap()

        w_out0b = B.dscr("w_out0b", [128, 12 * 1024], BF16)
        w1b = B.dscr("w1b", [2, 128, 8 * 4096], BF16)
        w2b = B.dscr("w2b", [2, 128, 32 * 1024], BF16)
        w_in1b = B.dscr("w_in1b", [128, 8 * 2560], BF16)
        w_out1b = B.dscr("w_out1b", [128, 12 * 1024], BF16)
        w_sTb = B.dscr("w_sTb", [128, 4 * 128], BF16)
        pool_wb = B.dscr("pool_wb", [128, 4 * 128], BF16)
        LAT = B.dscr("LAT", [128, 8, L], F32)
        XBC = B.dscr("XBC", [128, 12, L + 4], BF16)
        XBCC = B.dscr("XBCC", [128, 12, CTXL + 4], BF16)
        ZS = B.dscr("ZS", [128, 32, 1024], BF16)
        YF = B.dscr("YF", [128, 32, 1024], F32)
        VD = B.dscr("VD", [128, 32, 1024], BF16)
        MIXT = B.dscr("MIXT", [128, 12, L], BF16)
        dbg = {}
        if DEBUG:
            def dout(name, shape, dt):
                dbg[name] = nc.dram_tensor(name, shape, dt, kind="ExternalOutput").ap()
            dout("dbg_mod", [128, 192], F32)
            dout("dbg_dt", [128, 34 * 32], F32)
            dout("dbg_hst", [128, 2048], F32)
            dout("dbg_yf", [128, 32, 1024], F32)
            dout("dbg_mixt0", [128, 12, L], BF16)
            dout("dbg_mixt1", [128, 12, L], BF16)
            dout("dbg_lat1", [128, 8, L], F32)
            dout("dbg_lat2", [128, 8, L], F32)
            dout("dbg_lat3", [128, 8, L], F32)
            dout("dbg_xbc", [128, 12, L + 4], BF16)
            dout("dbg_vd", [128, 32, 1024], BF16)
            dout("dbg_zs", [128, 32, 1024], BF16)

        def snap(name, src):
            if DEBUG:
                B.dma('sp', dbg[name], src, r=[], w=[('dbg', name)])

        cols = B.sb("cols", [128, NCOLS], F32)
        ident = B.sb("ident_sb", [128, 128], F32)
        cb16 = B.sb("cb16_sb", [128, 256 + 4 * 128 + 128 + 128 + 20 * 128], BF16)
        dftd = cb16[:, 0:256]
        masks = cb16[:, 256:256 + 512].rearrange("p (a b) -> p a b", a=4)
        identb = cb16[:, 768:896]
        onesb = cb16[:, 896:1024]
        poolband = cb16[:, 1024:1024 + 20 * 128].rearrange("p (w v i) -> p w v i", w=4, v=5)
        rows_bc = B.sb("rows_bc", [128, 32 + 32 + 16 + 512 + 1024], F32)
        mod = B.sb("mod", [128, 2, 48, 2], F32)
        vecs = B.sb("vecs", [128, 64], F32)
        cvec = B.sb("cvec", [128, 8, 2], BF16)
        dt_all = B.sb("dt_all", [128, 34, 32], F32)
        a_bc = B.sb("a_bc", [128, 32], F32)
        zeros = B.sb("zeros", [128, 24], BF16)
        ps = B.es.enter_context(nc.psum_tensor("ps", [128, 8, 512], F32))
        B.arena_bytes = 186 * 1024
        B.arena = B.sb("arena", [128, B.arena_bytes // 2], BF16)

        def col(name, i):
            o, n = COLS[name]
            return cols[:, o + i:o + i + 1]

        def colr(name):
            o, n = COLS[name]
            return cols[:, o:o + n]

        def rbc(name):
            o, n = ROWS[name]
            return rows_bc[:, o:o + n]

        def vcol(base, i):
            return vecs[:, base + i:base + i + 1]

        def pbank(b):
            return ps[:, b, :]

        B.dma('sp', cols[:], cols_d, r=[], w=['cols'])
        B.dma('sp', ident[:], ident_d, r=[], w=['ident'])
        B.dma('sp', cb16[:], cb16_d, r=[], w=['cb16'])
        B.dma('sp', rows_bc[:], rows_d[0, 0:1616].partition_broadcast(128), r=[], w=['rows_bc'])
        B.memset('dve', zeros[:], 0.0, w=['zeros'])
        B.dma('sp', XBC[:, :, 0:2], zeros[:, 0:24].rearrange("p (a b) -> p a b", a=12), r=['zeros'], w=['XBCpad0'], nonc=True)
        B.dma('sp', XBC[:, :, L + 2:L + 4], zeros[:, 0:24].rearrange("p (a b) -> p a b", a=12), r=['zeros'], w=['XBCpad1'], nonc=True)
        B.dma('sp', XBCC[:, :, 0:2], zeros[:, 0:24].rearrange("p (a b) -> p a b", a=12), r=['zeros'], w=['XBCCpad0'], nonc=True)
        B.dma('sp', XBCC[:, :, CTXL + 2:CTXL + 4], zeros[:, 0:24].rearrange("p (a b) -> p a b", a=12), r=['zeros'], w=['XBCCpad1'], nonc=True)

        B.act(a_bc[:], rbc('alog'), AF.Exp, r=['rows_bc'], w=['a_bc0'])
        B.ts('dve', a_bc[:], a_bc[:], -1.0, None, ALU.mult, None, r=['a_bc0'], w=['a_bc'])

        B.act(cvec[:, :, 0], colr('cb'), AF.Silu, r=['cols'], w=['cvec0'])
        B.act(cvec[:, :, 1], colr('cctx'), AF.Silu, r=['cols'], w=['cvec1'])

        def adaln_load(l, blk, adab):
            nbuf = len(adab)
            t = adab[blk % nbuf]
            tk = ('adab', blk % nbuf)
            src = ada_w[l].rearrange("p (c n) -> p c n", c=8)[:, :, blk * 512:(blk + 1) * 512]
            B.dma('pool', t, src, r=[], w=[tk])

        def adaln_mm(l, blk, adab):
            nbuf = len(adab)
            t = adab[blk % nbuf]
            tk = ('adab', blk % nbuf)
            mb = B.nb()
            for fcl in range(4):
                for kc in range(8):
                    B.mm(ps[:, mb, fcl * 2: fcl * 2 + 2], t[:, kc, fcl * 128:(fcl + 1) * 128], cvec[:, kc, :],
                         kc == 0, kc == 7, r=[tk, 'cvec0', 'cvec1'], w=[('ps', mb)])
            o, n = COLS['adab%d' % l]
            B.tt('dve', mod[:, l, blk * 4:(blk + 1) * 4, :], ps[:, mb, 0:8].rearrange("p (a b) -> p a b", b=2),
                 cols[:, o + blk * 4:o + blk * 4 + 4].unsqueeze(2).to_broadcast([128, 4, 2]), ALU.add, r=[('ps', mb), 'cols'], w=[('mod', l, blk)])

        def adaln_block(l, blk, adab):
            adaln_load(l, blk, adab)
            adaln_mm(l, blk, adab)

        def adaln_finish(l):
            MK = [('mod', l, blk) for blk in range(12)] + ['cols']
            if l == 0:
                B.stt(vecs[:, 0:8], mod[:, 0, 8:16, 0], 1.0, colr('gmix0'), ALU.add, ALU.mult, r=MK, w=['vecs0'])
                B.stt(vecs[:, 8:16], mod[:, 0, 32:40, 0], 1.0, colr('gffn0'), ALU.add, ALU.mult, r=MK, w=['vecs1'])
                B.stt(vecs[:, 32:40], mod[:, 0, 8:16, 1], 1.0, colr('gmix0'), ALU.add, ALU.mult, r=MK, w=['vecs4'])
            else:
                B.stt(vecs[:, 16:24], mod[:, 1, 8:16, 0], 1.0, colr('gmix1'), ALU.add, ALU.mult, r=MK, w=['vecs2'])
                B.stt(vecs[:, 24:32], mod[:, 1, 32:40, 0], 1.0, colr('gffn1'), ALU.add, ALU.mult, r=MK, w=['vecs3'])

        B.arena_reset()
        adab0 = [B.carve([8, 512], BF16) for _ in range(3)]
        for blk in range(12):
            adaln_block(0, blk, adab0)
        adaln_finish(0)
        VK = []

        def modcol(l, which, j, fc):
            return mod[:, l, which * 8 + fc, j:j + 1]

        B.P.barrier()
        snap("dbg_mod", mod[:].rearrange("p a b c -> p (a b c)"))

        def norm_mod1(latblk, latkey, T, sq):
            for fc in range(8):
                B.act(sq[:, fc, 0:T], latblk[:, fc, 0:T], AF.Square, r=[latkey], w=[('sq', fc)])

        def norm_mod2(latblk, latkey, T, Abase, Sfn, hT, hkey, sq, rstd, lnv, tmp2):
            bk = B.nb()
            for fc in range(8):
                B.mm(ps[:, bk, 0:T], onesb, sq[:, fc, 0:T], fc == 0, fc == 7, r=[('sq', fc), 'cb16'], w=[('ps', bk)])
            B.act(lnv[:, 0:T], ps[:, bk, 0:T], AF.Ln, r=[('ps', bk)], w=['lnv'], scale=1.0 / DM, bias=EPS)
            B.act(rstd[:, 0:T], lnv[:, 0:T], AF.Exp, r=['lnv'], w=['rstd'], scale=-0.5)
            for fc in range(8):
                tm = tmp2[fc % 2]
                B.tt('dve', tm[:, 0:T], latblk[:, fc, 0:T], rstd[:, 0:T], ALU.mult, r=[latkey, 'rstd'], w=[('nmtmp', fc % 2)])
                B.act(hT[:, fc, 0:T], tm[:, 0:T], AF.Identity, r=[('nmtmp', fc % 2)] + VK, w=[hkey],
                      scale=vcol(Abase, fc), bias=Sfn(fc))

        def norm_mod(latblk, latkey, T, Abase, Sfn, hT, hkey, sq, rstd, lnv, tmp2):
            norm_mod1(latblk, latkey, T, sq)
            norm_mod2(latblk, latkey, T, Abase, Sfn, hT, hkey, sq, rstd, lnv, tmp2)

        def cast(dst, src, key):
            B.dma('pool', dst, src, r=[], w=[key], grp='cast')
        cast_jobs0 = [(w_out0b, w_out0, 'w_out0b')]
        for h in range(2):
            cast_jobs0.append((w1b[0][:, h * 4 * 4096:(h + 1) * 4 * 4096], w1[0][:, h * 4 * 4096:(h + 1) * 4 * 4096], ('w1b', 0, h)))
        for h in range(2):
            cast_jobs0.append((w2b[0][:, h * 16 * 1024:(h + 1) * 16 * 1024], w2[0][:, h * 16 * 1024:(h + 1) * 16 * 1024], ('w2b', 0, h)))
        cast_jobs1 = [(w_in1b, w_in1, 'w_in1b'), (w_sTb, w_sT, 'w_sTb'), (pool_wb, pool_w, 'pool_wb'), (w_out1b, w_out1, 'w_out1b')]
        for h in range(2):
            cast_jobs1.append((w1b[1][:, h * 4 * 4096:(h + 1) * 4 * 4096], w1[1][:, h * 4 * 4096:(h + 1) * 4 * 4096], ('w1b', 1, h)))
        for h in range(2):
            cast_jobs1.append((w2b[1][:, h * 16 * 1024:(h + 1) * 16 * 1024], w2[1][:, h * 16 * 1024:(h + 1) * 16 * 1024], ('w2b', 1, h)))

        B.arena_reset()
        B.ldq, B.stq = 'pool', 'sp'
        win = B.carve([8, 3104], BF16)
        xin = [B.carve([4, 1024], F32) for _ in range(2)]
        latblks = [B.carve([8, 512], F32) for _ in range(2)]
        sq = B.carve([8, 512], BF16)
        rstd = B.carve([512], F32)
        lnv = B.carve([512], F32)
        tmp2 = [B.carve([512], F32) for _ in range(2)]
        hTs = [B.carve([8, 512], BF16) for _ in range(2)]
        UT = B.carve([4, 512], BF16)
        xbc_o = B.carve([12, 512], BF16)
        zs_o = B.carve([4, 1024], BF16)
        v_o = B.carve([4, 1024], BF16)
        dtt = B.carve([32], F32)
        dtt2 = B.carve([32], F32)
        w_in0v = w_in0.rearrange("p (c n) -> p c n", c=8)
        for q in range(7):
            c0, c1 = q * 512, min((q + 1) * 512, 3104)
            B.dma('pool', win[:, :, c0:c1], w_in0v[:, :, c0:c1], r=[], w=[('win', q)])
        WINK = [('win', q) for q in range(7)]
        xv = x.rearrange("(t p) d -> p t d", p=128)
        cxv = ctx.rearrange("(t p) d -> p t d", p=128)
        evq = [0]

        def evac_copy(out_ap, in_ap, r, w):
            evq[0] += 1
            B.copy('dve' if evq[0] % 2 else 'act', out_ap, in_ap, r=r, w=w)

        def A_stageA(blk):
            isctx = blk == 8
            T = 256 if isctx else 512
            NTI = T // 128
            par = blk % 2
            xi = xin[par]
            xk = ('xin', par)
            latblk = latblks[par]
            lk = ('latblk', par)
            if isctx:
                B.ld(xi[:, 0:2, :], cxv, r=[], w=[xk])
            else:
                B.ld(xi, xv[:, blk * 4:(blk + 1) * 4, :], r=[], w=[xk])
            for fc in range(8):
                bk = B.nb()
                for t in range(NTI):
                    B.tr(ps[:, bk, t * 128:(t + 1) * 128], xi[:, t, fc * 128:(fc + 1) * 128], ident[:], r=[xk, 'ident'], w=[('ps', bk)])
                B.copy('dve', latblk[:, fc, 0:T], ps[:, bk, 0:T], r=[('ps', bk)], w=[lk])
            if not isctx:
                B.st(LAT[:, :, blk * 512:(blk + 1) * 512], latblk, r=[lk], w=[('LAT', blk)])
            if isctx:
                norm_mod(latblk, lk, T, 32, lambda fc: modcol(0, 0, 1, fc), hTs[par], ('hT', par), sq, rstd, lnv, tmp2)
            else:
                norm_mod(latblk, lk, T, 0, lambda fc: modcol(0, 0, 0, fc), hTs[par], ('hT', par), sq, rstd, lnv, tmp2)

        def A_stageB(blk):
            isctx = blk == 8
            T = 256 if isctx else 512
            NTI = T // 128
            par = blk % 2
            hT = hTs[par]
            hk = ('hT', par)
            ocs = list(range(12, 24)) if isctx else list(range(0, 4)) + list(range(12, 24))
            for oc in ocs:
                bk = B.nb()
                for kc in range(8):
                    B.mm(ps[:, bk, 0:T], win[:, kc, oc * 128:(oc + 1) * 128], hT[:, kc, 0:T], kc == 0, kc == 7, r=WINK + [hk], w=[('ps', bk)])
                if oc < 4:
                    evac_copy(UT[:, oc, 0:T], ps[:, bk, 0:T], r=[('ps', bk)], w=[('UT', oc)])
                else:
                    evac_copy(xbc_o[:, oc - 12, 0:T], ps[:, bk, 0:T], r=[('ps', bk)], w=[('xbc_o', oc - 12)])
            if isctx:
                B.st(XBCC[:, :, 2:2 + CTXL], xbc_o[:, :, 0:CTXL], r=[('xbc_o', j_) for j_ in range(12)], w=['XBCC'])
            else:
                B.st(XBC[:, :, 2 + blk * 512:2 + (blk + 1) * 512], xbc_o, r=[('xbc_o', j_) for j_ in range(12)], w=[('XBC', blk)])
            for t in range(NTI):
                if not isctx:
                    bA = B.nb()
                    bB = B.nb()
                    for g in range(4):
                        B.mm(ps[:, bA, g * 128:(g + 1) * 128], UT[:, g, t * 128:(t + 1) * 128], dftd[:, 0:128], True, True, r=[('UT', g), 'cb16'], w=[('ps', bA)])
                        B.mm(ps[:, bB, g * 128:(g + 1) * 128], UT[:, g, t * 128:(t + 1) * 128], dftd[:, 128:256], True, True, r=[('UT', g), 'cb16'], w=[('ps', bB)])
                    evac_copy(v_o[:, t, 0:512], ps[:, bA, :], r=[('ps', bA)], w=[('v_o', t, 0)])
                    evac_copy(v_o[:, t, 512:1024], ps[:, bB, :], r=[('ps', bB)], w=[('v_o', t, 1)])
                    for half in range(2):
                        bk = B.nb()
                        for kc in range(8):
                            B.mm(ps[:, bk, :], hT[:, kc, t * 128:(t + 1) * 128], win[:, kc, 512 + half * 512:1024 + half * 512], kc == 0, kc == 7, r=WINK + [hk], w=[('ps', bk)])
                        B.act(zs_o[:, t, half * 512:(half + 1) * 512], ps[:, bk, :], AF.Silu, r=[('ps', bk)], w=['zs_o'])
                bk = B.nb()
                for kc in range(8):
                    B.mm(ps[:, bk, 0:32], hT[:, kc, t * 128:(t + 1) * 128], win[:, kc, 3072:3104], kc == 0, kc == 7, r=WINK + [hk], w=[('ps', bk)])
                ti = (32 + t) if isctx else (blk * 4 + t)
                B.tt('dve', dtt, ps[:, bk, 0:32], rbc('dtbias'), ALU.add, r=[('ps', bk), 'rows_bc'], w=['dtt'])
                B.act(dtt2, dtt, AF.Exp, r=['dtt'], w=['dtt2'])
                B.act(dt_all[:, ti, :], dtt2, AF.Ln, r=['dtt2'], w=[('dt_all', ti)], bias=1.0)
            if not isctx:
                B.st(VD[:, blk * 4:(blk + 1) * 4, :], v_o, r=[('v_o', t_, h_) for t_ in range(4) for h_ in range(2)], w=[('VD', blk)])
                B.st(ZS[:, blk * 4:(blk + 1) * 4, :], zs_o, r=['zs_o'], w=[('ZS', blk)])

        A_stageA(0)
        for blk in range(9):
            if blk + 1 < 9:
                A_stageA(blk + 1)
            A_stageB(blk)
        B.P.barrier()
        snap("dbg_dt", dt_all[:].rearrange("p a b -> p (a b)"))
        snap("dbg_xbc", XBC)
        snap("dbg_vd", VD)
        snap("dbg_zs", ZS)

        B.arena_reset()
        V = B.carve([32, 1024], BF16)
        tcs = [B.carve([2, 4096], BF16) for _ in range(2)]
        four_all = B.carve([4, L], BF16)
        Qs = B.carve([512], F32)
        ysum = B.carve([512], BF16)
        ydif = B.carve([512], BF16)
        for q in range(4):
            B.ld(V[:, q * 8:(q + 1) * 8, :], VD[:, q * 8:(q + 1) * 8, :], r=[], w=[('V', q)])
        VKEYS = [('V', q) for q in range(4)]
        for kc in range(17):
            tb = tcs[kc % 2]
            tk = ('tcs', kc % 2)
            B.dma('sp' if kc % 2 else 'pool', tb, tcs_d[kc].rearrange("p (a b) -> p a b", a=2), r=[], w=[tk])
            bP = B.nb()
            bQ = B.nb()
            for a in range(32):
                B.mm(ps[:, bP, :], tb[:, 0, a * 128:(a + 1) * 128], V[:, a, 0:512], a == 0, a == 31, r=[tk] + VKEYS, w=[('ps', bP)])
            for a in range(32):
                B.mm(ps[:, bQ, :], tb[:, 1, a * 128:(a + 1) * 128], V[:, a, 512:1024], a == 0, a == 31, r=[tk] + VKEYS, w=[('ps', bQ)])
            B.copy('act', Qs, ps[:, bQ, :], r=[('ps', bQ)], w=['Qs'])
            B.tt('dve', ysum, ps[:, bP, :], Qs, ALU.add, r=[('ps', bP), 'Qs'], w=['ysum'])
            bt = B.nb()
            ptv = ps[:, bt, 0:256].bitcast(BF16)
            for g in range(4):
                B.tr(ptv[:, g * 128:(g + 1) * 128], ysum[:, g * 128:(g + 1) * 128], identb, r=['ysum', 'cb16'], w=[('ps', bt)])
            pt3 = ptv.rearrange("p (g i) -> p g i", g=4)
            if kc < 16:
                B.tt('dve', ydif, ps[:, bP, :], Qs, ALU.subtract, r=[('ps', bP), 'Qs'], w=['ydif'])
                B.copy('act', four_all[:, :, kc * 128:(kc + 1) * 128], pt3, r=[('ps', bt)], w=[('four', kc)])
                bt2 = B.nb()
                ptv2 = ps[:, bt2, 0:256].bitcast(BF16)
                for g in range(4):
                    B.tr(ptv2[:, g * 128:(g + 1) * 128], ydif[:, g * 128:(g + 1) * 128], identb, r=['ydif', 'cb16'], w=[('ps', bt2)])
                pt32 = ptv2.rearrange("p (g i) -> p g i", g=4)
                base = (31 - kc) * 128
                B.copy('dve', four_all[:, :, base + 127:base:-1], pt32[:, :, 1:128], r=[('ps', bt2)], w=[('fourm', kc)])
                if kc >= 1:
                    B.copy('dve', four_all[:, :, (32 - kc) * 128:(32 - kc) * 128 + 1], pt32[:, :, 0:1], r=[('ps', bt2)], w=[('fourm0', kc)])
            else:
                B.copy('act', four_all[:, :, 2048:2049], pt3[:, :, 0:1], r=[('ps', bt)], w=[('four', 16)])
        for q in range(4):
            FK = [('four', k_) for k_ in range(17)] + [('fourm', k_) for k_ in range(16)] + [('fourm0', k_) for k_ in range(1, 16)]
            B.st(MIXT[:, 0:4, q * 1024:(q + 1) * 1024], four_all[:, :, q * 1024:(q + 1) * 1024], r=FK, w=[('MIXTf', q)])
        B.P.barrier()

        B.arena_reset()
        B.ldq, B.stq = 'sp', 'pool'
        adab1 = [B.carve([8, 512], BF16) for _ in range(2)]
        Dg = B.carve([60, 128], BF16)
        xws = [B.carve([12, 516], BF16) for _ in range(2)]
        BCTs = [B.carve([4, 512], BF16) for _ in range(2)]
        xsbs = [B.carve([1280], BF16) for _ in range(2)]
        dtAs = [B.carve([16], BF16) for _ in range(2)]
        ectes = [B.carve([48], F32) for _ in range(2)]
        Rf = B.carve([2048], BF16)
        lndt = B.carve([16], F32)
        wtes = [B.carve([16], F32) for _ in range(2)]
        Decf = B.carve([2048], BF16)
        Wtfs = [B.carve([2048], BF16) for _ in range(2)]
        CBm = B.carve([2, 128], BF16)
        xdts = [B.carve([1024], BF16) for _ in range(2)]
        xdtte = B.carve([1024], BF16)
        hst = [B.carve([1024], F32) for _ in range(2)]
        hbf = B.carve([1024], BF16)
        t1s = [B.carve([1024], F32) for _ in range(2)]
        yaccs = [B.carve([1024], F32) for _ in range(2)]
        zss = [B.carve([1024], BF16) for _ in range(2)]
        junk = B.carve([512], BF16)
        ssq = B.carve([2], F32)
        rs2 = B.carve([2], F32)
        yns = [B.carve([1024], BF16) for _ in range(2)]
        ymix_sts = [B.carve([8, 512], BF16) for _ in range(2)]
        convb_row32 = B.carve([1536], F32)
        convb_row = B.carve([1536], BF16)
        XSB = B.dscr("XSB", [128, 32, 1280], BF16)
        BCTD = B.dscr("BCTD", [128, 8, 2048], BF16)
        B.ld(convb_row32[0:1, :], rows_d[0:1, ROWS['convb'][0]:ROWS['convb'][0] + 1536], r=[], w=['convb32'])
        B.copy('dve', convb_row[0:1, :], convb_row32[0:1, :], r=['convb32'], w=['convb_row'])
        o_cw, _ = COLS['convw']
        for k in range(5):
            for cc in range(12):
                B.ts('dve', Dg[:, k * 12 + cc, :], identb, cols[:, o_cw + k * 12 + cc:o_cw + k * 12 + cc + 1], None, ALU.mult, None,
                     r=['cb16', 'cols'], w=[('Dg', k, cc)])
        DGK = [('Dg', k, cc) for k in range(5) for cc in range(12)]
        B.memset('dve', hst[0], 0.0, w=[('hst', 0)])
        B.memset('dve', hst[1], 0.0, w=[('hst', 1)])
        ones_row = onesb[0:1, :]

        items = []
        for d in range(2):
            for c in ([0, 1] if d == 0 else [1, 0]):
                items.append(dict(d=d, ti=32 + c, ctx=True, blk=-1 - d, c=c, want_y=False, first=(c == (0 if d == 0 else 1)), last=False, ci=None))
        for d in range(2):
            blks = range(8) if d == 0 else range(7, -1, -1)
            for blk in blks:
                cs = list(range(4)) if d == 0 else list(range(3, -1, -1))
                for n_, c in enumerate(cs):
                    items.append(dict(d=d, ti=blk * 4 + c, ctx=False, blk=blk, c=c, want_y=True, first=(n_ == 0), last=(n_ == 3), ci=blk * 4 + c))
        bseq = -1
        for i, it in enumerate(items):
            if it['first']:
                bseq += 1
            it['bp'] = bseq % 2
            it['par'] = i % 2
            it['prev_item'] = items[i - 1] if i >= 1 else None
            it['start_dir'] = (i == 4) or (i == 4 + 32)

        def ctxv(it):
            d, ti, c, par, bp = it['d'], it['ti'], it['c'], it['par'], it['bp']
            v = dict(d=d, ti=ti, c=c, par=par, bp=bp)
            v['xw'] = xws[bp]; v['xwk'] = ('xw', bp)
            v['BCT'] = BCTs[bp]; v['bctk'] = ('BCT', bp)
            v['xsb'] = xsbs[par]; v['xs_tm'] = xsbs[par][:, 0:1024]; v['B_tm'] = xsbs[par][:, 1024:1280]; v['xsk'] = ('xsb', par)
            v['dtA'] = dtAs[par]; v['ecte'] = ectes[par]; v['ek'] = ('ecte', par)
            v['Mincl'] = masks[:, 0 + 2 * d, :]; v['Mstrict'] = masks[:, 1 + 2 * d, :]
            v['dt_c'] = dt_all[:, ti, d * 16:(d + 1) * 16]
            v['xdt'] = xdts[par]; v['t1'] = t1s[par]
            v['hs'] = hst[d]; v['hk'] = ('hst', d)
            v['T1K'] = [('t1', par, 0), ('t1', par, 1)]
            return v

        def S1a(it):
            v = ctxv(it)
            d, ti, c, par, bp = v['d'], v['ti'], v['c'], v['par'], v['bp']
            xw, xwk, BCT, bctk, xsb, xs_tm, B_tm, xsk = v['xw'], v['xwk'], v['BCT'], v['bctk'], v['xsb'], v['xs_tm'], v['B_tm'], v['xsk']
            dtA, ecte, Mincl, Mstrict, dt_c = v['dtA'], v['ecte'], v['Mincl'], v['Mstrict'], v['dt_c']
            c_off = c * 128
            if it['first']:
                if it['ctx']:
                    B.ld(xw[:, :, 0:CTXL + 4], XBCC, r=[], w=[xwk])
                elif d == 0:
                    blk = it['blk']
                    B.ld(xw, XBC[:, :, blk * 512:blk * 512 + 516], r=[], w=[xwk])
                    for cc in range(8, 12):
                        bk = B.nb()
                        for k in range(5):
                            B.mm(ps[:, bk, :], Dg[:, k * 12 + cc, :], xw[:, cc, k:k + 512], k == 0, k == 4, r=[xwk] + DGK, w=[('ps', bk)])
                        B.act(BCT[:, cc - 8, :], ps[:, bk, :], AF.Silu, r=[('ps', bk)], w=[bctk], bias=col('convb', cc))
                else:
                    blk = it['blk']
                    B.ld(BCT.rearrange("p a b -> p (a b)"), BCTD[:, blk, :], r=[('BCTD', blk)], w=[bctk])
            if it['ctx'] or d == 0:
                for grp in range(3):
                    ccs = [0, 1, 2, 3] if grp == 0 else ([4, 5, 6, 7] if grp == 1 else [8, 9])
                    bk = B.nb()
                    for cc in ccs:
                        dst = ps[:, bk, (cc % 4) * 128:(cc % 4 + 1) * 128]
                        for k in range(5):
                            B.mm(dst, xw[:, cc, c_off + k:c_off + k + 128], Dg[:, k * 12 + cc, :], k == 0, False, r=[xwk] + DGK, w=[('ps', bk)])
                        B.mm(dst, ones_row, convb_row[0:1, cc * 128:(cc + 1) * 128], False, True, r=['cb16', 'convb_row'], w=[('ps', bk)])
                    if grp < 2:
                        B.act(xs_tm[:, grp * 512:(grp + 1) * 512], ps[:, bk, :], AF.Silu, r=[('ps', bk)], w=[xsk])
                    else:
                        B.act(B_tm, ps[:, bk, 0:256], AF.Silu, r=[('ps', bk)], w=[xsk])
            else:
                B.ld(xsb, XSB[:, it['ci'], :], r=[('XSB', it['ci'])], w=[xsk])
                B.ld(zss[par], ZS[:, it['ci'], :], r=[], w=[('zs', par)])
            B.tt('dve', dtA, dt_c, a_bc[:, d * 16:(d + 1) * 16], ALU.mult, r=[('dt_all', ti), 'a_bc'], w=[('dtA', par)])
            bk = B.nb()
            B.mm(ps[:, bk, 0:16], Mincl, dtA, True, True, r=[('dtA', par), 'cb16'], w=[('ps', bk)])
            B.mm(ps[:, bk, 16:32], Mstrict, dtA, True, True, r=[('dtA', par), 'cb16'], w=[('ps', bk)])
            B.mm(ps[:, bk, 32:48], onesb, dtA, True, True, r=[('dtA', par), 'cb16'], w=[('ps', bk)])
            B.act(ecte, ps[:, bk, 0:48], AF.Exp, r=[('ps', bk)], w=[('ecte', par)])
            if it['want_y']:
                R3 = Rf.rearrange("p (h i) -> p h i", h=16)
                B.tt('dve', R3[:, 0:8, :], Mincl.unsqueeze(1).to_broadcast([128, 8, 128]),
                     dtA[:, 0:8].unsqueeze(2).to_broadcast([128, 8, 128]), ALU.mult, r=[('dtA', par), 'cb16'], w=[('R', 0)])
                B.tt('pool', R3[:, 8:16, :], Mincl.unsqueeze(1).to_broadcast([128, 8, 128]),
                     dtA[:, 8:16].unsqueeze(2).to_broadcast([128, 8, 128]), ALU.mult, r=[('dtA', par), 'cb16'], w=[('R', 1)])
                B.act(lndt, dt_c, AF.Ln, r=[('dt_all', ti)], w=['lndt'])
                for q in range(4):
                    b = B.nb()
                    B.mm(ps[:, b, :], Mstrict, Rf[:, 512 * q:512 * (q + 1)], True, True, r=[('R', q // 2), 'cb16'], w=[('ps', b)])
                    for hh in range(4):
                        h_ = 4 * q + hh
                        B.act(Decf[:, 128 * h_:128 * (h_ + 1)], ps[:, b, 128 * hh:128 * (hh + 1)], AF.Exp, r=[('ps', b), 'lndt'], w=[('Dec', h_)],
                              bias=lndt[:, h_:h_ + 1])
                bcb = B.nb()
                it['bcb'] = bcb
                for g in range(2):
                    B.mm(ps[:, bcb, g * 128:(g + 1) * 128], BCT[:, g, c * 128:(c + 1) * 128], BCT[:, 2 + g, c * 128:(c + 1) * 128], True, True,
                         r=[bctk], w=[('ps', bcb)])
                B.tt('dve', CBm, ps[:, bcb, 0:256].rearrange("p (g i) -> p g i", g=2), Mincl.unsqueeze(1).to_broadcast([128, 2, 128]),
                     ALU.mult, r=[('ps', bcb), 'cb16'], w=['CBm'])

        def S1b(it):
            v = ctxv(it)
            d, ti, par = v['d'], v['ti'], v['par']
            B.tt('dve', wtes[par], v['dt_c'], v['ecte'][:, 16:32], ALU.mult, r=[('dt_all', ti), v['ek']], w=[('wte', par)])
            if it['want_y']:
                if d == 0:
                    B.tt('pool', yaccs[par].rearrange("p (h q) -> p h q", h=16), v['xs_tm'].rearrange("p (h q) -> p h q", h=16),
                         rbc('dskip').unsqueeze(2).to_broadcast([128, 16, 64]), ALU.mult, r=[v['xsk'], 'rows_bc'], w=[('yacc', par)])
                    B.st(XSB[:, it['ci'], :], v['xsb'], r=[v['xsk']], w=[('XSB', it['ci'])])
                    if it['first']:
                        B.st(BCTD[:, it['blk'], :], v['BCT'].rearrange("p a b -> p (a b)"), r=[v['bctk']], w=[('BCTD', it['blk'])])

        def S1c(it):
            if not it['want_y']:
                return
            par = it['par']
            Wt3 = Wtfs[par].rearrange("p (h i) -> p h i", h=16)
            Dec3 = Decf.rearrange("p (h i) -> p h i", h=16)
            for g in range(2):
                B.tt('dve', Wt3[:, 8 * g:8 * g + 8, :], Dec3[:, 8 * g:8 * g + 8, :], CBm[:, g, :].unsqueeze(1).to_broadcast([128, 8, 128]),
                     ALU.mult, r=[('Dec', 8 * g + j_) for j_ in range(8)] + ['CBm'], w=[('Wt', par, g)])

        def S2a(it):
            v = ctxv(it)
            d, c, par = v['d'], v['c'], v['par']
            BCT, bctk, B_tm, xsk, ecte, ek, xdt, hs, hk = v['BCT'], v['bctk'], v['B_tm'], v['xsk'], v['ecte'], v['ek'], v['xdt'], v['hs'], v['hk']
            if it['start_dir']:
                B.copy('act', hbf, hs, r=[hk], w=['hbf'])
            pv = it.get('prev_item')
            if pv is not None and pv['want_y'] and pv['d'] == 0:
                ppar = pv['par']
                B.st(YF[:, pv['ci'], :], t1s[ppar], r=[('t1', ppar, 0), ('t1', ppar, 1)], w=[('YF', pv['ci'])])
            if it['want_y'] and d == 1:
                B.ld(yaccs[par], YF[:, it['ci'], :], r=[('YF', it['ci'])], w=[('yacc', par)])
            B.tt('pool', xdtte.rearrange("p (h q) -> p h q", h=16), v['xs_tm'].rearrange("p (h q) -> p h q", h=16),
                 wtes[par].unsqueeze(2).to_broadcast([128, 16, 64]), ALU.mult, r=[xsk, ('wte', par)], w=['xdtte'])
            if it['want_y']:
                ob0 = B.nb2()
                ob = [ob0, ob0 + 1]
                for g in range(2):
                    B.mm(ps[:, ob[g], :], BCT[:, 2 + g, c * 128:(c + 1) * 128], hbf[:, g * 512:(g + 1) * 512], True, True,
                         r=[bctk, 'hbf'], w=[('ps', ob[g])])
                it['ob'] = ob
            sb0 = B.nb2()
            sbk = [sb0, sb0 + 1]
            for g in range(2):
                sl = slice(g * 512, (g + 1) * 512)
                B.mm(ps[:, sbk[g], :], B_tm[:, g * 128:(g + 1) * 128], xdtte[:, sl], True, True, r=[xsk, 'xdtte'], w=[('ps', sbk[g])])
            it['sbk'] = sbk
            B.tt('dve', hs.rearrange("p (h q) -> p h q", h=16), hs.rearrange("p (h q) -> p h q", h=16),
                 ecte[:, 32:48].unsqueeze(2).to_broadcast([128, 16, 64]), ALU.mult, r=[hk, ek], w=[hk])
            B.tt('dve', hs, hs, ps[:, sb0:sb0 + 2, :].rearrange("p a n -> p (a n)"), ALU.add, r=[hk, ('ps', sbk[0]), ('ps', sbk[1])], w=[hk])
            B.copy('act', hbf, hs, r=[hk], w=['hbf'])
            if it['want_y']:
                Wt3 = Wtfs[par].rearrange("p (h i) -> p h i", h=16)
                yb0 = B.nb2()
                yb = [yb0, yb0 + 1]
                for h in range(16):
                    B.mm(ps[:, yb[h // 8], (h % 8) * 64:(h % 8 + 1) * 64], Wt3[:, h, :], v['xs_tm'][:, h * 64:(h + 1) * 64], True, True,
                         r=[('Wt', par, h // 8), xsk], w=[('ps', yb[h // 8])])
                it['yb'] = yb

        def S2b(it):
            v = ctxv(it)
            d, par = v['d'], v['par']
            ecte, ek, hs, hk, t1, T1K = v['ecte'], v['ek'], v['hs'], v['hk'], v['t1'], v['T1K']
            if it['want_y']:
                yb, ob = it['yb'], it['ob']
                B.tt('dve', t1.rearrange("p (h q) -> p h q", h=16), ps[:, ob[0]:ob[0] + 2, :].rearrange("p a (h q) -> p (a h) q", q=64),
                     ecte[:, 0:16].unsqueeze(2).to_broadcast([128, 16, 64]), ALU.mult, r=[('ps', ob[0]), ('ps', ob[1]), ek], w=T1K)
                B.tt('dve', t1, t1, ps[:, yb[0]:yb[0] + 2, :].rearrange("p a n -> p (a n)"), ALU.add, r=T1K + [('ps', yb[0]), ('ps', yb[1])], w=T1K)
            if it['want_y']:
                B.tt('dve', t1, t1, yaccs[par], ALU.add, r=T1K + [('yacc', par)], w=T1K)

        def S2c(it):
            if not (it['want_y'] and it['d'] == 1):
                return
            v = ctxv(it)
            par, t1, T1K = v['par'], v['t1'], v['T1K']
            B.tt('dve', t1, t1, zss[par], ALU.mult, r=T1K + [('zs', par)], w=T1K)
            for g in range(2):
                B.act(junk, t1[:, g * 512:(g + 1) * 512], AF.Square, r=T1K, w=['junk', ('ssq', g)], accum_out=ssq[:, g:g + 1])
            B.act(rs2, ssq, AF.Ln, r=[('ssq', 0), ('ssq', 1)], w=['rs2a'], scale=1.0 / 512, bias=EPS)
            B.act(rs2, rs2, AF.Exp, r=['rs2a'], w=['rs2'], scale=-0.5)
            for g in range(2):
                B.act(yns[par][:, g * 512:(g + 1) * 512], t1[:, g * 512:(g + 1) * 512], AF.Copy, r=T1K + ['rs2'], w=[('yn', par, g)], scale=rs2[:, g:g + 1])

        def S2t(it):
            if not (it['want_y'] and it['d'] == 1):
                return
            par, bp, c = it['par'], it['bp'], it['c']
            ymix_st = ymix_sts[bp]
            for half in range(2):
                bt = B.nb()
                ptv = ps[:, bt, 0:256].bitcast(BF16)
                for j in range(4):
                    fc = half * 4 + j
                    B.tr(ptv[:, j * 128:(j + 1) * 128], yns[par][:, fc * 128:(fc + 1) * 128], identb, r=[('yn', par, half), 'cb16'], w=[('ps', bt)])
                B.copy('act', ymix_st[:, half * 4:half * 4 + 4, c * 128:(c + 1) * 128], ptv.rearrange("p (g i) -> p g i", g=4),
                       r=[('ps', bt)], w=[('ymix_st', bp)])
            if it['last']:
                blk = it['blk']
                B.st(MIXT[:, 4:12, blk * 512:(blk + 1) * 512], ymix_st, r=[('ymix_st', bp)], w=[('MIXTy', blk)])

        NI = len(items)
        cj = list(cast_jobs0)
        cj1s = list(cast_jobs1)
        ada_next = 0
        for i in range(NI + 2):
            nxt = items[i] if i < NI else None
            cur = items[i - 1] if 1 <= i <= NI else None
            prv = items[i - 2] if 2 <= i <= NI + 1 else None
            if nxt is not None:
                S1a(nxt)
            if cur is not None:
                S2a(cur)
            if prv is not None:
                S2t(prv)
            if nxt is not None:
                S1b(nxt)
            if cur is not None:
                S2b(cur)
            if nxt is not None:
                S1c(nxt)
            if cur is not None:
                S2c(cur)
            if i >= 4 and i % 5 == 4 and ada_next < 12:
                if ada_next == 0:
                    adaln_load(1, 0, adab1)
                else:
                    adaln_mm(1, ada_next - 1, adab1)
                    if ada_next < 12:
                        adaln_load(1, ada_next, adab1)
                ada_next += 1
            if i >= 6 and i % 8 == 6 and cj:
                cast(*cj.pop(0))
            if i >= 42 and i % 3 == 0 and cj1s:
                cast(*cj1s.pop(0))
        while ada_next < 12:
            if ada_next >= 1:
                adaln_mm(1, ada_next - 1, adab1)
            adaln_load(1, ada_next, adab1)
            ada_next += 1
        adaln_mm(1, 11, adab1)
        adaln_finish(1)
        for j in cj + cj1s:
            cast(*j)
        B.ldq, B.stq = 'pool', 'sp'
        B.P.barrier()

        def outproj(layer, woutb, gfold):
            B.arena_reset()
            wo = B.carve([12, 1024], BF16)
            mixb = [B.carve([12, 512], BF16) for _ in range(2)]
            lb = [B.carve([8, 512], F32) for _ in range(2)]
            B.dma('sp', wo, woutb.rearrange("p (c n) -> p c n", c=12), r=['w_out0b' if layer == 0 else 'w_out1b'], w=['wo'])
            if gfold:
                for fc in range(8):
                    B.ts('dve', wo[:, 4 + fc, :], wo[:, 4 + fc, :], col('ssdg', fc), None, ALU.mult, None, r=['wo', 'cols'], w=['wo'])
            for blk in range(8):
                mb = mixb[blk % 2]
                mk = ('mixb', blk % 2)
                l_ = lb[blk % 2]
                lk = ('lb', blk % 2)
                B.ld(mb, MIXT[:, :, blk * 512:(blk + 1) * 512], r=[], w=[mk])
                B.ld(l_, LAT[:, :, blk * 512:(blk + 1) * 512], r=[], w=[lk])
                for oc in range(8):
                    bk = B.nb()
                    for kc in range(12):
                        B.mm(ps[:, bk, :], wo[:, kc, oc * 128:(oc + 1) * 128], mb[:, kc, :], kc == 0, kc == 11, r=['wo', mk], w=[('ps', bk)])
                    B.stt(l_[:, oc, :], ps[:, bk, :], modcol(layer, 2, 0, oc), l_[:, oc, :], ALU.mult, ALU.add, r=[('ps', bk), lk] + VK, w=[lk])
                B.st(LAT[:, :, blk * 512:(blk + 1) * 512], l_, r=[lk], w=[('LATo', blk)])
            B.P.barrier()
            snap("dbg_mixt%d" % layer, MIXT)
            snap("dbg_lat%d" % (1 if layer == 0 else 3), LAT)
            B.P.barrier()

        def ffn(layer, final):
            B.arena_reset()
            W1 = B.carve([8, 4096], BF16)
            W2 = B.carve([32, 1024], BF16)
            TB = 256
            NBK = L // TB
            lbks = [B.carve([8, TB], F32) for _ in range(2)]
            hfs = [B.carve([8, TB], BF16) for _ in range(2)]
            sqf = B.carve([8, TB], BF16)
            rstdf = B.carve([TB], F32)
            lnvf = B.carve([TB], F32)
            tmpn = [B.carve([TB], F32) for _ in range(2)]
            tmpr = [B.carve([TB], F32) for _ in range(2)]
            uflat = B.carve([32 * TB], BF16)
            u = uflat.rearrange("p (a b) -> p a b", a=32)
            otm = B.carve([2, 1024], F32) if final else None
            OTMK = ['otm']
            for h in range(2):
                B.dma('sp', W1[:, h * 4:(h + 1) * 4, :], w1b[layer].rearrange("p (c n) -> p c n", c=8)[:, h * 4:(h + 1) * 4, :], r=[('w1b', layer, 0), ('w1b', layer, 1)], w=[('W1', h)])
            for h in range(4):
                B.dma('sp', W2[:, h * 8:(h + 1) * 8, :], w2b[layer].rearrange("p (c n) -> p c n", c=32)[:, h * 8:(h + 1) * 8, :], r=[('w2b', layer, 0), ('w2b', layer, 1)], w=[('W2', h)])
            W1K = [('W1', 0), ('W1', 1)]
            W2K = [('W2', h) for h in range(4)]
            Abase = 8 if layer == 0 else 24

            def stageA1(blk):
                par = blk % 2
                tsl = slice(blk * TB, (blk + 1) * TB)
                B.ld(lbks[par], LAT[:, :, tsl], r=[], w=[('lbk', par)])
                norm_mod1(lbks[par], ('lbk', par), TB, sqf)

            def stageA2(blk):
                par = blk % 2
                norm_mod2(lbks[par], ('lbk', par), TB, Abase, lambda fc: modcol(layer, 3, 0, fc), hfs[par], ('hf', par), sqf, rstdf, lnvf, tmpn)

            def stageB(blk, nxt):
                par = blk % 2
                lbk = lbks[par]
                lk = ('lbk', par)
                hf = hfs[par]
                hk = ('hf', par)
                tsl = slice(blk * TB, (blk + 1) * TB)
                for fc in range(32):
                    bk = B.nb()
                    for kc in range(8):
                        B.mm(ps[:, bk, 0:TB], W1[:, kc, fc * 128:(fc + 1) * 128], hf[:, kc, :], kc == 0, kc == 7, r=W1K + [hk], w=[('ps', bk)])
                    B.act(tmpr[fc % 2], ps[:, bk, 0:TB], AF.Relu, r=[('ps', bk)], w=[('rl', fc % 2)])
                    B.tt('dve', u[:, fc, :], tmpr[fc % 2], tmpr[fc % 2], ALU.mult, r=[('rl', fc % 2)], w=[('u', fc)])
                    if final and blk >= 1 and fc == 10:
                        stageF1(blk - 1)
                UK = [('u', fc) for fc in range(32)]
                if final and blk >= 1:
                    stageF2(blk - 1)
                if nxt is not None:
                    stageA1(nxt)
                for oc in range(8):
                    if oc == 3 and nxt is not None:
                        stageA2(nxt)
                    bk = B.nb()
                    for fc in range(32):
                        B.mm(ps[:, bk, 0:TB], W2[:, fc, oc * 128:(oc + 1) * 128], u[:, fc, :], fc == 0, fc == 31, r=W2K + [('u', fc)], w=[('ps', bk)])
                    B.stt(lbk[:, oc, :], ps[:, bk, 0:TB], modcol(layer, 5, 0, oc), lbk[:, oc, :], ALU.mult, ALU.add, r=[('ps', bk), lk], w=[lk])
                if not final:
                    B.st(LAT[:, :, tsl], lbk, r=[lk], w=[('LATf', blk)])

            def stageF1(blk):
                par = blk % 2
                lbk = lbks[par]
                lk = ('lbk', par)
                if True:
                    for fc in range(8):
                        B.act(sqf[:, fc, :], lbk[:, fc, :], AF.Square, r=[lk], w=[('sq', fc)])
                    bk = B.nb()
                    for fc in range(8):
                        B.mm(ps[:, bk, 0:TB], onesb, sqf[:, fc, :], fc == 0, fc == 7, r=[('sq', fc), 'cb16'], w=[('ps', bk)])
                    B.act(lnvf, ps[:, bk, 0:TB], AF.Ln, r=[('ps', bk)], w=['lnv'], scale=1.0 / DM, bias=EPS)
                    B.act(rstdf, lnvf, AF.Exp, r=['lnv'], w=['rstd'], scale=-0.5)
                    for fc in range(8):
                        B.stt(lbk[:, fc, :], lbk[:, fc, :], col('fing', fc), rstdf, ALU.mult, ALU.mult, r=[lk, 'rstd', 'cols'], w=[lk])

            def stageF2(blk):
                par = blk % 2
                lbk = lbks[par]
                lk = ('lbk', par)
                if True:
                    for t in range(TB // 128):
                        for half in range(2):
                            bk = B.nb()
                            for j in range(4):
                                fc = half * 4 + j
                                B.tr(ps[:, bk, j * 128:(j + 1) * 128], lbk[:, fc, t * 128:(t + 1) * 128], ident[:], r=[lk, 'ident'], w=[('ps', bk)])
                            evac_copy(otm[:, t, half * 512:(half + 1) * 512], ps[:, bk, :], r=[('ps', bk)], w=[('otm', t, half)])
                    B.st(out.rearrange("(t p) d -> p t d", p=128)[:, blk * 2:(blk + 1) * 2, :], otm, r=[('otm', t_, h_) for t_ in range(TB // 128) for h_ in range(2)], w=[('out', blk)])

            cj1 = []
            stageA1(0)
            stageA2(0)
            for blk in range(NBK):
                stageB(blk, blk + 1 if blk + 1 < NBK else None)
            if final:
                stageF1(NBK - 1)
                stageF2(NBK - 1)
                if cj1:
                    cast(*cj1.pop(0))
            for j in cj1:
                cast(*j)
            if final:
                B.P.add('sp', lambda e: e.nop(), r=[('out', blk) for blk in range(NBK)])
            B.P.barrier()
            if not final:
                snap("dbg_lat2", LAT)
                B.P.barrier()

        outproj(0, w_out0b, True)
        ffn(0, False)

        B.arena_reset()
        B.ldq, B.stq = 'pool', 'sp'
        win1 = B.carve([8, 2560], BF16)
        wsT = B.carve([4, 128], BF16)
        pw = B.carve([4, 128], BF16)
        bias2 = B.carve([8, 128], F32)
        latbs = [B.carve([8, 512], F32) for _ in range(2)]
        sq1 = B.carve([8, 512], BF16)
        rstd1 = B.carve([512], F32)
        lnv1 = B.carve([512], F32)
        tmp1 = [B.carve([512], F32) for _ in range(2)]
        h1s = [B.carve([8, 512], BF16) for _ in range(2)]
        uT = B.carve([8, 512], BF16)
        vgs = [B.carve([1024], F32) for _ in range(2)]
        vhats = [B.carve([1024], BF16) for _ in range(2)]
        bsts = [B.carve([2, 6], F32) for _ in range(2)]
        mvs = [B.carve([2], F32) for _ in range(2)]
        rsvs = [B.carve([1], F32) for _ in range(2)]
        p_all = B.carve([32, 512], BF16)
        vm = B.carve([512], F32)
        sgu_st = B.carve([8, 512], BF16)
        pl = B.carve([4, 128], BF16)
        pool_st = B.carve([4, 512], BF16)
        B.dma('sp', win1, w_in1b.rearrange("p (c n) -> p c n", c=8), r=['w_in1b'], w=['win1'])
        B.dma('sp', wsT, w_sTb.rearrange("p (g i) -> p g i", g=4), r=['w_sTb'], w=['wsT'])
        B.dma('sp', pw, pool_wb.rearrange("p (g i) -> p g i", g=4), r=['pool_wb'], w=['pw'])
        for g in range(4):
            bk = B.nb()
            B.mm(ps[:, bk, 0:128], onesb, wsT[:, g, :], True, True, r=['wsT', 'cb16'], w=[('ps', bk)])
            o_bs, _ = ROWS['bs']
            for j in range(2):
                fc = g * 2 + j
                B.stt(bias2[:, fc, :], ps[:, bk, 0:128], col('lnb', fc), rows_bc[:, o_bs + g * 128:o_bs + (g + 1) * 128], ALU.mult, ALU.add,
                      r=[('ps', bk), 'cols', 'rows_bc'], w=['bias2'])

        def E_stageA(blk):
            par = blk % 2
            B.ld(latbs[par], LAT[:, :, blk * 512:(blk + 1) * 512], r=[], w=[('latb', par)])
            norm_mod(latbs[par], ('latb', par), 512, 16, lambda fc: modcol(1, 0, 0, fc), h1s[par], ('h1', par), sq1, rstd1, lnv1, tmp1)

        def E_VP(blk, t):
            h1 = h1s[blk % 2]
            hk = ('h1', blk % 2)
            vp = t % 2
            vg = vgs[vp]
            ti = blk * 4 + t
            for half in range(2):
                bk = B.nb()
                for kc in range(8):
                    B.mm(ps[:, bk, :], h1[:, kc, t * 128:(t + 1) * 128], win1[:, kc, 1024 + half * 512:1536 + half * 512], kc == 0, kc == 7,
                         r=['win1', hk], w=[('ps', bk)])
                B.act(vg[:, half * 512:(half + 1) * 512], ps[:, bk, :], AF.Gelu_apprx_tanh, r=[('ps', bk)], w=[('vg', vp, half)])
                B.P.add('dve', (lambda hh, vv, bb: (lambda e: e.bn_stats(out=bb[:, hh, :], in_=vv[:, hh * 512:(hh + 1) * 512])))(half, vg, bsts[vp]),
                        r=[('vg', vp, half)], w=[('bst', vp, half)])
            B.P.add('dve', (lambda mm_, bb: (lambda e: e.bn_aggr(out=mm_, in_=bb)))(mvs[vp], bsts[vp]), r=[('bst', vp, 0), ('bst', vp, 1)], w=[('mv', vp)])
            B.act(rsvs[vp], mvs[vp][:, 1:2], AF.Ln, r=[('mv', vp)], w=[('rsva', vp)], bias=EPS)
            B.act(rsvs[vp], rsvs[vp], AF.Exp, r=[('rsva', vp)], w=[('rsv', vp)], scale=-0.5)
            VGK = [('vg', vp, 0), ('vg', vp, 1)]
            B.ts('dve', vg, vg, mvs[vp][:, 0:1], rsvs[vp][:, 0:1], ALU.subtract, ALU.mult, r=VGK + [('mv', vp), ('rsv', vp)], w=VGK)
            B.tt('dve', vhats[vp], vg, rbc('lng'), ALU.mult, r=VGK + ['rows_bc'], w=[('vhat', vp)])

        def E_PP(blk, t):
            h1 = h1s[blk % 2]
            hk = ('h1', blk % 2)
            ti = blk * 4 + t
            bk = B.nb()
            for kc in range(8):
                B.mm(ps[:, bk, :], h1[:, kc, t * 128:(t + 1) * 128], win1[:, kc, 2048:2560], kc == 0, kc == 7, r=['win1', hk], w=[('ps', bk)])
            evac_copy(p_all[:, ti, :], ps[:, bk, :], r=[('ps', bk)], w=[('p_all', ti)])

        def E_SP(blk, t):
            vp = t % 2
            vhat = vhats[vp]
            for half in range(2):
                bk = B.nb()
                for j in range(4):
                    fc = half * 4 + j
                    B.mm(ps[:, bk, j * 128:(j + 1) * 128], vhat[:, fc * 128:(fc + 1) * 128], wsT[:, fc // 2, :], True, True, r=[('vhat', vp), 'wsT'], w=[('ps', bk)])
                B.tt('dve', vm.rearrange("p (a b) -> p a b", a=4), ps[:, bk, :].rearrange("p (a b) -> p a b", a=4), bias2[:, half * 4:half * 4 + 4, :],
                     ALU.add, r=[('ps', bk), 'bias2'], w=['vm'])
                B.tt('dve', sgu_st[:, half * 4:half * 4 + 4, t * 128:(t + 1) * 128], vm.rearrange("p (a b) -> p a b", a=4),
                     uT[:, half * 4:half * 4 + 4, t * 128:(t + 1) * 128], ALU.mult, r=['vm'] + [('uT', half * 4 + j) for j in range(4)], w=['sgu_st'])

        def E_stageB(blk):
            h1 = h1s[blk % 2]
            hk = ('h1', blk % 2)
            E_VP(blk, 0)
            for oc in range(8):
                bk = B.nb()
                for kc in range(8):
                    B.mm(ps[:, bk, :], win1[:, kc, oc * 128:(oc + 1) * 128], h1[:, kc, :], kc == 0, kc == 7, r=['win1', hk], w=[('ps', bk)])
                B.act(uT[:, oc, :], ps[:, bk, :], AF.Gelu_apprx_tanh, r=[('ps', bk)], w=[('uT', oc)])
            for t in range(1, 5):
                if t < 4:
                    E_VP(blk, t)
                else:
                    for tt_ in range(4):
                        E_PP(blk, tt_)
                E_SP(blk, t - 1)
            B.st(MIXT[:, 0:8, blk * 512:(blk + 1) * 512], sgu_st, r=['sgu_st'], w=[('MIXTs', blk)])

        E_stageA(0)
        for blk in range(8):
            if blk + 1 < 8:
                E_stageA(blk + 1)
            E_stageB(blk)
        pls = [pl, B.carve([4, 128], BF16)]

        def pool_band(ti):
            bk = B.nb()
            for g in range(4):
                terms = []
                if ti > 0:
                    terms.append((ti - 1, 1))
                terms.append((ti, 3 if ti == 0 else (4 if ti == 31 else 0)))
                if ti < 31:
                    terms.append((ti + 1, 2))
                for n_, (jt, var) in enumerate(terms):
                    B.mm(ps[:, bk, g * 128:(g + 1) * 128], p_all[:, jt, g * 128:(g + 1) * 128], poolband[:, g, var, :], n_ == 0, n_ == len(terms) - 1,
                         r=[('p_all', jt), 'cb16'], w=[('ps', bk)])
            B.copy('dve', pls[ti % 2], ps[:, bk, :].rearrange("p (g i) -> p g i", g=4), r=[('ps', bk)], w=[('pl', ti % 2)])

        def pool_w_(ti):
            bk2 = B.nb()
            for g in range(4):
                B.mm(ps[:, bk2, g * 128:(g + 1) * 128], pw[:, g, :], pls[ti % 2][:, g, :], True, True, r=['pw', ('pl', ti % 2)], w=[('ps', bk2)])
            for g in range(4):
                dst = pool_st[:, g, (ti % 4) * 128:(ti % 4 + 1) * 128]
                if g % 2 == 0:
                    B.act(dst, ps[:, bk2, g * 128:(g + 1) * 128], AF.Copy, r=[('ps', bk2), 'cols'], w=[('pool_st', g)], scale=col('pscale', g))
                else:
                    B.ts('dve', dst, ps[:, bk2, g * 128:(g + 1) * 128], col('pscale', g), None, ALU.mult, None, r=[('ps', bk2), 'cols'], w=[('pool_st', g)])
            if ti % 4 == 3:
                blk = ti // 4
                B.st(MIXT[:, 8:12, blk * 512:(blk + 1) * 512], pool_st, r=[('pool_st', g_) for g_ in range(4)], w=[('MIXTp', blk)])

        for ti in range(33):
            if ti < 32:
                pool_band(ti)
            if ti >= 1:
                pool_w_(ti - 1)
        B.P.barrier()

        outproj(1, w_out1b, False)
        ffn(1, True)

        if DEBUG:
            B.P.barrier()
        B.P.emit()
        B.es.close()
        return nc


_CACHE = {}


def _consts():
    if 'c' in _CACHE:
        return _CACHE['c']
    bf = ml_dtypes.bfloat16
    ident = np.eye(128, dtype=np.float32)
    d = np.arange(128)
    ang = 2 * np.pi * np.outer(d, d) / 128.0
    dftd = np.concatenate([np.cos(ang), -np.sin(ang)], axis=1) / np.sqrt(128.0)
    k = np.arange(128)[:, None]
    i = np.arange(128)[None, :]
    masks = np.stack([(k <= i), (k > i), (k >= i), (k < i)], axis=1).astype(np.float32)
    onesm = np.ones((128, 128), np.float32)
    pb = np.zeros((128, 4, 5, 128), np.float32)

    def blockT(w, jt, it):
        r = w // 2
        jj = (jt * 128 + np.arange(128))[:, None]
        ii = (it * 128 + np.arange(128))[None, :]
        cnt = np.minimum(ii + r + 1, L) - np.maximum(ii - r, 0)
        return (np.abs(ii - jj) <= r).astype(np.float64) / cnt - (ii == jj)

    for wi, w in enumerate((2, 4, 8, 16)):
        pb[:, wi, 0, :] = blockT(w, 5, 5)
        pb[:, wi, 1, :] = blockT(w, 4, 5)
        pb[:, wi, 2, :] = blockT(w, 6, 5)
        pb[:, wi, 3, :] = blockT(w, 0, 0)
        pb[:, wi, 4, :] = blockT(w, 31, 31)
    cb16 = np.concatenate([dftd, masks.reshape(128, 512), ident, onesm, pb.reshape(128, 20 * 128)], axis=1).astype(bf)
    p = np.arange(128)
    a = np.arange(32)
    l_idx = (a[None, :] * 128 + p[:, None])
    tcs = np.empty((32, 128, 2, 4096), dtype=bf)
    for kc in range(32):
        kk = kc * 128 + np.arange(128)
        prod = (l_idx[:, :, None] * kk[None, None, :]) % L
        angp = prod.astype(np.float64) * (2 * np.pi / L)
        tcs[kc, :, 0, :] = (np.cos(angp) / 64.0).reshape(128, 4096).astype(bf)
        tcs[kc, :, 1, :] = (np.sin(angp) / 64.0).reshape(128, 4096).astype(bf)
    _CACHE['c'] = dict(ident=ident, cb16=cb16, tcs=tcs.reshape(32, 128, 8192))
    return _CACHE['c']


def _pk(w, kc):
    n = w.shape[1]
    return np.ascontiguousarray(w.reshape(kc, 128, n).transpose(1, 0, 2).reshape(128, kc * n))


def _colv(v):
    return np.ascontiguousarray(v.reshape(-1, 128).T)


def kernel(x, c, ctx, c_ctx, ada_w, ada_b, mix_norm_g, ffn_norm_g, ffn_w1, ffn_w2,
           ev_w_in, ev_conv_w, ev_conv_b, ev_dt_bias, ev_a_log, ev_d_skip, ev_ssd_norm_g, ev_w_out,
           od_w_in, od_sgu_ln_g, od_sgu_ln_b, od_w_s, od_b_s, od_pool_w, od_pool_scale, od_w_out,
           final_g):
    f = lambda a: np.asarray(a, dtype=np.float32)
    x, c, ctx, c_ctx = f(x), f(c), f(ctx), f(c_ctx)
    cs = _consts()
    if 'nc' not in _CACHE:
        _CACHE['nc'] = Builder().build()
    nc = _CACHE['nc']
    shared = {
        "ada_w": np.stack([_pk(f(ada_w)[l], 8) for l in range(2)]),
        "w_in0": _pk(f(ev_w_in)[0], 8),
        "w_out0": _pk(f(ev_w_out)[0], 12),
        "w1": np.stack([_pk(f(ffn_w1)[l], 8) for l in range(2)]),
        "w2": np.stack([_pk(f(ffn_w2)[l], 32) for l in range(2)]),
        "w_in1": _pk(f(od_w_in)[0], 8),
        "w_out1": _pk(f(od_w_out)[0], 12),
        "w_sT": np.ascontiguousarray(f(od_w_s)[0].transpose(2, 0, 1).reshape(128, 512)),
        "pool_w": np.ascontiguousarray(f(od_pool_w)[0].transpose(1, 0, 2).reshape(128, 512)),
        "ident": cs['ident'], "cb16": cs['cb16'], "tcs": cs['tcs'],
    }
    rows = np.zeros((1, NROWS), np.float32)
    for name, v in [('dtbias', f(ev_dt_bias)[0].reshape(-1)), ('alog', f(ev_a_log)[0].reshape(-1)), ('dskip', f(ev_d_skip)[0]),
                    ('bs', f(od_b_s)[0].reshape(-1)), ('lng', f(od_sgu_ln_g)[0]), ('convb', f(ev_conv_b)[0])]:
        o, n = ROWS[name]
        rows[0, o:o + n] = v
    in_maps = []
    for b in range(8):
        cols = np.zeros((128, NCOLS), np.float32)

        def put(name, arr):
            o, n = COLS[name]
            cols[:, o:o + n] = arr
        put('gmix0', _colv(f(mix_norm_g)[0])); put('gmix1', _colv(f(mix_norm_g)[1]))
        put('gffn0', _colv(f(ffn_norm_g)[0])); put('gffn1', _colv(f(ffn_norm_g)[1]))
        put('adab0', _colv(f(ada_b)[0])); put('adab1', _colv(f(ada_b)[1]))
        cw = f(ev_conv_w)[0]
        put('convw', np.concatenate([_colv(cw[k]) for k in range(5)], axis=1))
        put('convb', _colv(f(ev_conv_b)[0]))
        put('ssdg', _colv(f(ev_ssd_norm_g)[0]))
        put('pscale', _colv(f(od_pool_scale)[0]))
        put('fing', _colv(f(final_g)))
        put('cb', _colv(c[b])); put('cctx', _colv(c_ctx))
        put('lnb', _colv(f(od_sgu_ln_b)[0]))
        m = dict(shared)
        m["x"] = np.ascontiguousarray(x[b])
        m["ctx"] = np.ascontiguousarray(ctx[b])
        m["cols"] = cols
        m["rows"] = rows
        in_maps.append(m)
    res = run_bass_kernel_spmd(nc, in_maps, core_ids=list(range(8)))
    return np.stack([np.asarray(r["out"], dtype=np.float32) for r in res.results], axis=0)
```

```python
import contextlib
import numpy as np
import ml_dtypes
import concourse.bass as bass
import concourse.mybir as mybir
from concourse.bass_utils import run_bass_kernel_spmd

F32 = mybir.dt.float32
BF16 = mybir.dt.bfloat16
AF = mybir.ActivationFunctionType
ALU = mybir.AluOpType

L = 4096
DM = 1024
CTXL = 256
EPS = 1e-6
DEBUG = False


class Prog:
    COMPUTE = ('pe', 'act', 'dve', 'pool')

    def __init__(self, nc, ndma_sems=8):
        self.nc = nc
        self.ops = []
        self.last_w = {}
        self.readers = {}
        self.ndma = ndma_sems
        self.grp_n = {'cast': 13}

    def add(self, eng, fn, r=(), w=(), dma=False, grp=None):
        idx = len(self.ops)
        deps = set()
        for k in r:
            if k in self.last_w:
                deps.add(self.last_w[k])
            if isinstance(k, tuple) and k[0] == 'ps':
                for rd in self.readers.get(k, ()):
                    if self.ops[rd][0] != eng:
                        deps.add(rd)
        for k in w:
            if k in self.last_w:
                deps.add(self.last_w[k])
            for rd in self.readers.get(k, ()):
                deps.add(rd)
        self.ops.append([eng, fn, deps, dma, (grp or eng) if dma else None])
        for k in w:
            self.last_w[k] = idx
            self.readers[k] = []
        for k in r:
            lst = self.readers.setdefault(k, [])
            if not dma:
                for j in range(len(lst) - 1, -1, -1):
                    o = self.ops[lst[j]]
                    if (not o[3]) and o[0] == eng:
                        del lst[j]
            lst.append(idx)
        return idx

    def barrier(self):
        n = len(self.ops)
        deps = set()
        seen_c = set()
        dcount = {}
        for i in range(n - 1, max(-1, n - 20000), -1):
            eng, fn, d, dma, grp = self.ops[i]
            if dma:
                if grp != 'cast' and dcount.get(grp, 0) < self.ndma:
                    deps.add(i)
                    dcount[grp] = dcount.get(grp, 0) + 1
            elif eng not in seen_c:
                seen_c.add(eng)
                deps.add(i)
        b0 = len(self.ops)
        self.ops.append(['act', lambda e: e.nop(), deps, False, None])
        for eng in ('pe', 'dve', 'pool', 'sp', 'act'):
            self.ops.append([eng, lambda e: e.nop(), {b0}, False, None])
        keep = {k: v for k, v in self.last_w.items() if self.ops[v][4] == 'cast'}
        self.last_w = keep
        self.readers = {}

    def emit(self):
        nc = self.nc
        ops = self.ops
        n = len(ops)
        engs = ['pe', 'act', 'dve', 'pool', 'sp']
        need = [False] * n
        for i, (eng, fn, deps, dma, grp) in enumerate(ops):
            for d in deps:
                de, _, _, ddma, _g = ops[d]
                if (not ddma) and de == 'pe' and eng == 'pe' and not dma:
                    continue
                need[d] = True
        val = [0] * n
        semof = [None] * n
        cnt = {e: 0 for e in engs}
        dcnt = {}
        slotcnt = {}
        prev_on_slot = [None] * n
        last_on_slot = {}
        for i, (eng, fn, deps, dma, grp) in enumerate(ops):
            if dma:
                ns = self.grp_n.get(grp, self.ndma)
                s = dcnt.get(grp, 0) % ns
                dcnt[grp] = dcnt.get(grp, 0) + 1
                key = (grp, s)
                slotcnt[key] = slotcnt.get(key, 0) + 16
                val[i] = slotcnt[key]
                semof[i] = ('d', grp, s)
                prev_on_slot[i] = last_on_slot.get(key)
                last_on_slot[key] = i
            elif need[i]:
                cnt[eng] += 1
                val[i] = cnt[eng]
                semof[i] = ('c', eng, 0)
        self.stats = dict(n=n, incs=dict(cnt), dmas=dict(dcnt))
        with contextlib.ExitStack() as es:
            csem = {e: es.enter_context(nc.semaphore(f"c_{e}")) for e in engs}
            dsem = {}
            for g in dcnt:
                for s in range(self.grp_n.get(g, self.ndma)):
                    dsem[(g, s)] = es.enter_context(nc.semaphore(f"d_{g}{s}"))
            block = es.enter_context(nc.Block())
            per_eng = {e: [i for i in range(n) if ops[i][0] == e] for e in engs}
            nwaits = [0]

            def semh(so):
                return csem[so[1]] if so[0] == 'c' else dsem[(so[1], so[2])]

            def body(e, eh):
                known = {}
                for i in per_eng[e]:
                    eng, fn, deps, dma, grp = ops[i]
                    req = {}
                    dl = list(deps)
                    if dma and prev_on_slot[i] is not None:
                        dl.append(prev_on_slot[i])
                    for d in dl:
                        de, _, _, ddma, _g = ops[d]
                        if (not ddma) and de == 'pe' and eng == 'pe' and not dma:
                            continue
                        so = semof[d]
                        if val[d] > req.get(so, 0):
                            req[so] = val[d]
                    for so, v in req.items():
                        if known.get(so, 0) >= v:
                            continue
                        eh.wait_ge(semh(so), v)
                        known[so] = v
                        nwaits[0] += 1
                    ins = fn(eh)
                    if dma:
                        ins.then_inc(semh(semof[i]), 16)
                    elif need[i]:
                        ins.then_inc(semh(semof[i]), 1)

            if per_eng['pe']:
                @block.tensor
                def _(eh):
                    body('pe', eh)
            if per_eng['act']:
                @block.scalar
                def _(eh):
                    body('act', eh)
            if per_eng['dve']:
                @block.vector
                def _(eh):
                    body('dve', eh)
            if per_eng['pool']:
                @block.gpsimd
                def _(eh):
                    body('pool', eh)
            if per_eng['sp']:
                @block.sync
                def _(eh):
                    body('sp', eh)
            self.stats['waits'] = nwaits[0]


COLS = {}
_c = 0
for _name, _n in [('gmix0', 8), ('gmix1', 8), ('gffn0', 8), ('gffn1', 8), ('adab0', 48), ('adab1', 48),
                  ('convw', 60), ('convb', 12), ('ssdg', 8), ('pscale', 4), ('fing', 8), ('cb', 8), ('cctx', 8),
                  ('lnb', 8)]:
    COLS[_name] = (_c, _n)
    _c += _n
NCOLS = _c
ROWS = {}
_c = 0
for _name, _n in [('dtbias', 32), ('alog', 32), ('dskip', 16), ('bs', 512), ('lng', 1024), ('convb', 1536)]:
    ROWS[_name] = (_c, _n)
    _c += _n
NROWS = _c


class Builder:
    def __init__(self):
        self.nc = bass.Bass("TRN2", target_bir_lowering=False)
        self.P = Prog(self.nc)
        self.bank_ctr = 0
        self.es = contextlib.ExitStack()
        self.uid = 0
        self.ldq = 'pool'
        self.stq = 'sp'

    def mm(self, out, lhsT, rhs, start, stop, r, w):
        self.P.add('pe', lambda e: e.matmul(out, lhsT=lhsT, rhs=rhs, start=start, stop=stop), r=r, w=w)

    def tr(self, out, in_, ident, r, w):
        self.P.add('pe', lambda e: e.transpose(out=out, in_=in_, identity=ident), r=r, w=w)

    def act(self, out, in_, func, r, w, scale=None, bias=None, accum_out=None):
        kw = {}
        if scale is not None:
            kw['scale'] = scale
        if bias is not None:
            kw['bias'] = bias
        if accum_out is not None:
            kw['accum_out'] = accum_out
        self.P.add('act', lambda e: e.activation(out=out, in_=in_, func=func, **kw), r=r, w=w)

    def tt(self, eng, out, in0, in1, op, r, w):
        self.P.add(eng, lambda e: e.tensor_tensor(out=out, in0=in0, in1=in1, op=op), r=r, w=w)

    def ts(self, eng, out, in0, s1, s2, op0, op1, r, w):
        if op1 is None:
            self.P.add(eng, lambda e: e.tensor_scalar(out=out, in0=in0, scalar1=s1, scalar2=None, op0=op0), r=r, w=w)
        else:
            self.P.add(eng, lambda e: e.tensor_scalar(out=out, in0=in0, scalar1=s1, scalar2=s2, op0=op0, op1=op1), r=r, w=w)

    def stt(self, out, in0, scalar, in1, op0, op1, r, w):
        self.P.add('dve', lambda e: e.scalar_tensor_tensor(out=out, in0=in0, scalar=scalar, in1=in1, op0=op0, op1=op1), r=r, w=w)

    def copy(self, eng, out, in_, r, w):
        if eng == 'act':
            self.P.add('act', lambda e: e.activation(out=out, in_=in_, func=AF.Copy), r=r, w=w)
        else:
            self.P.add(eng, lambda e: e.tensor_copy(out=out, in_=in_), r=r, w=w)

    def memset(self, eng, ap, val, w):
        self.P.add(eng, lambda e: e.memset(ap, val), w=w)

    def dma(self, q, out, in_, r, w, nonc=False, grp=None):
        nc = self.nc
        if nonc:
            def f(e):
                with nc.allow_non_contiguous_dma(reason="small strided"):
                    return e.dma_start(out=out, in_=in_)
        else:
            def f(e):
                return e.dma_start(out=out, in_=in_)
        self.P.add(q, f, r=r, w=w, dma=True, grp=grp)

    def ld(self, out, in_, r, w, nonc=False):
        self.dma(self.ldq, out, in_, r, w, nonc=nonc)

    def st(self, out, in_, r, w, nonc=False):
        self.dma(self.stq, out, in_, r, w, nonc=nonc)

    def nb(self):
        b = self.bank_ctr % 8
        self.bank_ctr += 1
        return b

    def nb2(self):
        if self.bank_ctr % 2:
            self.bank_ctr += 1
        b = self.bank_ctr % 8
        self.bank_ctr += 2
        return b

    def key(self, s):
        self.uid += 1
        return (s, self.uid)

    def din(self, name, shape, dt=F32):
        return self.nc.dram_tensor(name, shape, dt, kind="ExternalInput").ap()

    def dscr(self, name, shape, dt):
        return self.nc.dram_tensor(name, shape, dt, kind="Internal").ap()

    def sb(self, name, shape, dt):
        return self.es.enter_context(self.nc.sbuf_tensor("sb_" + name, shape, dt))

    def arena_reset(self):
        self.aoff = 0

    def carve(self, shape, dt):
        n = 1
        for s in shape:
            n *= s
        nbytes = n * (4 if dt == F32 else 2)
        nbytes = (nbytes + 63) // 64 * 64
        off = self.aoff
        self.aoff += nbytes
        assert self.aoff <= self.arena_bytes, (self.aoff, self.arena_bytes)
        a = self.arena[:, off // 2: off // 2 + (n * (2 if dt == F32 else 1))]
        if dt == F32:
            a = a.bitcast(F32)
        if len(shape) == 1:
            return a
        if len(shape) == 2:
            return a.rearrange("p (a b) -> p a b", a=shape[0])
        if len(shape) == 3:
            return a.rearrange("p (a b c) -> p a b c", a=shape[0], b=shape[1])
        raise ValueError

    def build(self):
        nc = self.nc
        B = self
        x = B.din("x", [L, DM])
        ctx = B.din("ctx", [CTXL, DM])
        cols_d = B.din("cols", [128, NCOLS])
        rows_d = B.din("rows", [1, NROWS])
        ada_w = B.din("ada_w", [2, 128, 8 * 6144])
        w_in0 = B.din("w_in0", [128, 8 * 3104])
        w_out0 = B.din("w_out0", [128, 12 * 1024])
        w1 = B.din("w1", [2, 128, 8 * 4096])
        w2 = B.din("w2", [2, 128, 32 * 1024])
        w_in1 = B.din("w_in1", [128, 8 * 2560])
        w_out1 = B.din("w_out1", [128, 12 * 1024])
        w_sT = B.din("w_sT", [128, 4 * 128])
        pool_w = B.din("pool_w", [128, 4 * 128])
        ident_d = B.din("ident", [128, 128])
        cb16_d = B.din("cb16", [128, 256 + 4 * 128 + 128 + 128 + 20 * 128], BF16)
        tcs_d = B.din("tcs", [32, 128, 2 * 4096], BF16)
        out = nc.dram_tensor("out", [L, DM], F32, kind="ExternalOutput").ap()

        w_out0b = B.dscr("w_out0b", [128, 12 * 1024], BF16)
        w1b = B.dscr("w1b", [2, 128, 8 * 4096], BF16)
        w2b = B.dscr("w2b", [2, 128, 32 * 1024], BF16)
        w_in1b = B.dscr("w_in1b", [128, 8 * 2560], BF16)
        w_out1b = B.dscr("w_out1b", [128, 12 * 1024], BF16)
        w_sTb = B.dscr("w_sTb", [128, 4 * 128], BF16)
        pool_wb = B.dscr("pool_wb", [128, 4 * 128], BF16)
        LAT = B.dscr("LAT", [128, 8, L], F32)
        XBC = B.dscr("XBC", [128, 12, L + 4], BF16)
        XBCC = B.dscr("XBCC", [128, 12, CTXL + 4], BF16)
        ZS = B.dscr("ZS", [128, 32, 1024], BF16)
        YF = B.dscr("YF", [128, 32, 1024], F32)
        VD = B.dscr("VD", [128, 32, 1024], BF16)
        MIXT = B.dscr("MIXT", [128, 12, L], BF16)
        dbg = {}
        if DEBUG:
            def dout(name, shape, dt):
                dbg[name] = nc.dram_tensor(name, shape, dt, kind="ExternalOutput").ap()
            dout("dbg_mod", [128, 192], F32)
            dout("dbg_dt", [128, 34 * 32], F32)
            dout("dbg_hst", [128, 2048], F32)
            dout("dbg_yf", [128, 32, 1024], F32)
            dout("dbg_mixt0", [128, 12, L], BF16)
            dout("dbg_mixt1", [128, 12, L], BF16)
            dout("dbg_lat1", [128, 8, L], F32)
            dout("dbg_lat2", [128, 8, L], F32)
            dout("dbg_lat3", [128, 8, L], F32)
            dout("dbg_xbc", [128, 12, L + 4], BF16)
            dout("dbg_vd", [128, 32, 1024], BF16)
            dout("dbg_zs", [128, 32, 1024], BF16)

        def snap(name, src):
            if DEBUG:
                B.dma('sp', dbg[name], src, r=[], w=[('dbg', name)])

        cols = B.sb("cols", [128, NCOLS], F32)
        ident = B.sb("ident_sb", [128, 128], F32)
        cb16 = B.sb("cb16_sb", [128, 256 + 4 * 128 + 128 + 128 + 20 * 128], BF16)
        dftd = cb16[:, 0:256]
        masks = cb16[:, 256:256 + 512].rearrange("p (a b) -> p a b", a=4)
        identb = cb16[:, 768:896]
        onesb = cb16[:, 896:1024]
        poolband = cb16[:, 1024:1024 + 20 * 128].rearrange("p (w v i) -> p w v i", w=4, v=5)
        rows_bc = B.sb("rows_bc", [128, 32 + 32 + 16 + 512 + 1024], F32)
        mod = B.sb("mod", [128, 2, 48, 2], F32)
        vecs = B.sb("vecs", [128, 64], F32)
        cvec = B.sb("cvec", [128, 8, 2], BF16)
        dt_all = B.sb("dt_all", [128, 34, 32], F32)
        a_bc = B.sb("a_bc", [128, 32], F32)
        zeros = B.sb("zeros", [128, 24], BF16)
        ps = B.es.enter_context(nc.psum_tensor("ps", [128, 8, 512], F32))
        B.arena_bytes = 186 * 1024
        B.arena = B.sb("arena", [128, B.arena_bytes // 2], BF16)

        def col(name, i):
            o, n = COLS[name]
            return cols[:, o + i:o + i + 1]

        def colr(name):
            o, n = COLS[name]
            return cols[:, o:o + n]

        def rbc(name):
            o, n = ROWS[name]
            return rows_bc[:, o:o + n]

        def vcol(base, i):
            return vecs[:, base + i:base + i + 1]

        def pbank(b):
            return ps[:, b, :]

        B.dma('sp', cols[:], cols_d, r=[], w=['cols'])
        B.dma('sp', ident[:], ident_d, r=[], w=['ident'])
        B.dma('sp', cb16[:], cb16_d, r=[], w=['cb16'])
        B.dma('sp', rows_bc[:], rows_d[0, 0:1616].partition_broadcast(128), r=[], w=['rows_bc'])
        B.memset('dve', zeros[:], 0.0, w=['zeros'])
        B.dma('sp', XBC[:, :, 0:2], zeros[:, 0:24].rearrange("p (a b) -> p a b", a=12), r=['zeros'], w=['XBCpad0'], nonc=True)
        B.dma('sp', XBC[:, :, L + 2:L + 4], zeros[:, 0:24].rearrange("p (a b) -> p a b", a=12), r=['zeros'], w=['XBCpad1'], nonc=True)
        B.dma('sp', XBCC[:, :, 0:2], zeros[:, 0:24].rearrange("p (a b) -> p a b", a=12), r=['zeros'], w=['XBCCpad0'], nonc=True)
        B.dma('sp', XBCC[:, :, CTXL + 2:CTXL + 4], zeros[:, 0:24].rearrange("p (a b) -> p a b", a=12), r=['zeros'], w=['XBCCpad1'], nonc=True)

        B.act(a_bc[:], rbc('alog'), AF.Exp, r=['rows_bc'], w=['a_bc0'])
        B.ts('dve', a_bc[:], a_bc[:], -1.0, None, ALU.mult, None, r=['a_bc0'], w=['a_bc'])

        B.act(cvec[:, :, 0], colr('cb'), AF.Silu, r=['cols'], w=['cvec0'])
        B.act(cvec[:, :, 1], colr('cctx'), AF.Silu, r=['cols'], w=['cvec1'])

        def adaln_load(l, blk, adab):
            nbuf = len(adab)
            t = adab[blk % nbuf]
            tk = ('adab', blk % nbuf)
            src = ada_w[l].rearrange("p (c n) -> p c n", c=8)[:, :, blk * 512:(blk + 1) * 512]
            B.dma('pool', t, src, r=[], w=[tk])

        def adaln_mm(l, blk, adab):
            nbuf = len(adab)
            t = adab[blk % nbuf]
            tk = ('adab', blk % nbuf)
            mb = B.nb()
            for fcl in range(4):
                for kc in range(8):
                    B.mm(ps[:, mb, fcl * 2: fcl * 2 + 2], t[:, kc, fcl * 128:(fcl + 1) * 128], cvec[:, kc, :],
                         kc == 0, kc == 7, r=[tk, 'cvec0', 'cvec1'], w=[('ps', mb)])
            o, n = COLS['adab%d' % l]
            B.tt('dve', mod[:, l, blk * 4:(blk + 1) * 4, :], ps[:, mb, 0:8].rearrange("p (a b) -> p a b", b=2),
                 cols[:, o + blk * 4:o + blk * 4 + 4].unsqueeze(2).to_broadcast([128, 4, 2]), ALU.add, r=[('ps', mb), 'cols'], w=[('mod', l, blk)])

        def adaln_block(l, blk, adab):
            adaln_load(l, blk, adab)
            adaln_mm(l, blk, adab)

        def adaln_finish(l):
            MK = [('mod', l, blk) for blk in range(12)] + ['cols']
            if l == 0:
                B.stt(vecs[:, 0:8], mod[:, 0, 8:16, 0], 1.0, colr('gmix0'), ALU.add, ALU.mult, r=MK, w=['vecs0'])
                B.stt(vecs[:, 8:16], mod[:, 0, 32:40, 0], 1.0, colr('gffn0'), ALU.add, ALU.mult, r=MK, w=['vecs1'])
                B.stt(vecs[:, 32:40], mod[:, 0, 8:16, 1], 1.0, colr('gmix0'), ALU.add, ALU.mult, r=MK, w=['vecs4'])
            else:
                B.stt(vecs[:, 16:24], mod[:, 1, 8:16, 0], 1.0, colr('gmix1'), ALU.add, ALU.mult, r=MK, w=['vecs2'])
                B.stt(vecs[:, 24:32], mod[:, 1, 32:40, 0], 1.0, colr('gffn1'), ALU.add, ALU.mult, r=MK, w=['vecs3'])

        B.arena_reset()
        adab0 = [B.carve([8, 512], BF16) for _ in range(3)]
        for blk in range(12):
            adaln_block(0, blk, adab0)
        adaln_finish(0)
        VK = []

        def modcol(l, which, j, fc):
            return mod[:, l, which * 8 + fc, j:j + 1]

        B.P.barrier()
        snap("dbg_mod", mod[:].rearrange("p a b c -> p (a b c)"))

        def norm_mod1(latblk, latkey, T, sq):
            for fc in range(8):
                B.act(sq[:, fc, 0:T], latblk[:, fc, 0:T], AF.Square, r=[latkey], w=[('sq', fc)])

        def norm_mod2(latblk, latkey, T, Abase, Sfn, hT, hkey, sq, rstd, lnv, tmp2):
            bk = B.nb()
            for fc in range(8):
                B.mm(ps[:, bk, 0:T], onesb, sq[:, fc, 0:T], fc == 0, fc == 7, r=[('sq', fc), 'cb16'], w=[('ps', bk)])
            B.act(lnv[:, 0:T], ps[:, bk, 0:T], AF.Ln, r=[('ps', bk)], w=['lnv'], scale=1.0 / DM, bias=EPS)
            B.act(rstd[:, 0:T], lnv[:, 0:T], AF.Exp, r=['lnv'], w=['rstd'], scale=-0.5)
            for fc in range(8):
                tm = tmp2[fc % 2]
                B.tt('dve', tm[:, 0:T], latblk[:, fc, 0:T], rstd[:, 0:T], ALU.mult, r=[latkey, 'rstd'], w=[('nmtmp', fc % 2)])
                B.act(hT[:, fc, 0:T], tm[:, 0:T], AF.Identity, r=[('nmtmp', fc % 2)] + VK, w=[hkey],
                      scale=vcol(Abase, fc), bias=Sfn(fc))

        def norm_mod(latblk, latkey, T, Abase, Sfn, hT, hkey, sq, rstd, lnv, tmp2):
            norm_mod1(latblk, latkey, T, sq)
            norm_mod2(latblk, latkey, T, Abase, Sfn, hT, hkey, sq, rstd, lnv, tmp2)

        def cast(dst, src, key):
            B.dma('pool', dst, src, r=[], w=[key], grp='cast')
        cast_jobs0 = [(w_out0b, w_out0, 'w_out0b')]
        for h in range(2):
            cast_jobs0.append((w1b[0][:, h * 4 * 4096:(h + 1) * 4 * 4096], w1[0][:, h * 4 * 4096:(h + 1) * 4 * 4096], ('w1b', 0, h)))
        for h in range(2):
            cast_jobs0.append((w2b[0][:, h * 16 * 1024:(h + 1) * 16 * 1024], w2[0][:, h * 16 * 1024:(h + 1) * 16 * 1024], ('w2b', 0, h)))
        cast_jobs1 = [(w_in1b, w_in1, 'w_in1b'), (w_sTb, w_sT, 'w_sTb'), (pool_wb, pool_w, 'pool_wb'), (w_out1b, w_out1, 'w_out1b')]
        for h in range(2):
            cast_jobs1.append((w1b[1][:, h * 4 * 4096:(h + 1) * 4 * 4096], w1[1][:, h * 4 * 4096:(h + 1) * 4 * 4096], ('w1b', 1, h)))
        for h in range(2):
            cast_jobs1.append((w2b[1][:, h * 16 * 1024:(h + 1) * 16 * 1024], w2[1][:, h * 16 * 1024:(h + 1) * 16 * 1024], ('w2b', 1, h)))

        B.arena_reset()
        B.ldq, B.stq = 'pool', 'sp'
        win = B.carve([8, 3104], BF16)
        xin = [B.carve([4, 1024], F32) for _ in range(2)]
        latblks = [B.carve([8, 512], F32) for _ in range(2)]
        sq = B.carve([8, 512], BF16)
        rstd = B.carve([512], F32)
        lnv = B.carve([512], F32)
        tmp2 = [B.carve([512], F32) for _ in range(2)]
        hTs = [B.carve([8, 512], BF16) for _ in range(2)]
        UT = B.carve([4, 512], BF16)
        xbc_o = B.carve([12, 512], BF16)
        zs_o = B.carve([4, 1024], BF16)
        v_o = B.carve([4, 1024], BF16)
        dtt = B.carve([32], F32)
        dtt2 = B.carve([32], F32)
        w_in0v = w_in0.rearrange("p (c n) -> p c n", c=8)
        for q in range(7):
            c0, c1 = q * 512, min((q + 1) * 512, 3104)
            B.dma('pool', win[:, :, c0:c1], w_in0v[:, :, c0:c1], r=[], w=[('win', q)])
        WINK = [('win', q) for q in range(7)]
        xv = x.rearrange("(t p) d -> p t d", p=128)
        cxv = ctx.rearrange("(t p) d -> p t d", p=128)
        evq = [0]

        def evac_copy(out_ap, in_ap, r, w):
            evq[0] += 1
            B.copy('dve' if evq[0] % 2 else 'act', out_ap, in_ap, r=r, w=w)

        def A_stageA(blk):
            isctx = blk == 8
            T = 256 if isctx else 512
            NTI = T // 128
            par = blk % 2
            xi = xin[par]
            xk = ('xin', par)
            latblk = latblks[par]
            lk = ('latblk', par)
            if isctx:
                B.ld(xi[:, 0:2, :], cxv, r=[], w=[xk])
            else:
                B.ld(xi, xv[:, blk * 4:(blk + 1) * 4, :], r=[], w=[xk])
            for fc in range(8):
                bk = B.nb()
                for t in range(NTI):
                    B.tr(ps[:, bk, t * 128:(t + 1) * 128], xi[:, t, fc * 128:(fc + 1) * 128], ident[:], r=[xk, 'ident'], w=[('ps', bk)])
                B.copy('dve', latblk[:, fc, 0:T], ps[:, bk, 0:T], r=[('ps', bk)], w=[lk])
            if not isctx:
                B.st(LAT[:, :, blk * 512:(blk + 1) * 512], latblk, r=[lk], w=[('LAT', blk)])
            if isctx:
                norm_mod(latblk, lk, T, 32, lambda fc: modcol(0, 0, 1, fc), hTs[par], ('hT', par), sq, rstd, lnv, tmp2)
            else:
                norm_mod(latblk, lk, T, 0, lambda fc: modcol(0, 0, 0, fc), hTs[par], ('hT', par), sq, rstd, lnv, tmp2)

        def A_stageB(blk):
            isctx = blk == 8
            T = 256 if isctx else 512
            NTI = T // 128
            par = blk % 2
            hT = hTs[par]
            hk = ('hT', par)
            ocs = list(range(12, 24)) if isctx else list(range(0, 4)) + list(range(12, 24))
            for oc in ocs:
                bk = B.nb()
                for kc in range(8):
                    B.mm(ps[:, bk, 0:T], win[:, kc, oc * 128:(oc + 1) * 128], hT[:, kc, 0:T], kc == 0, kc == 7, r=WINK + [hk], w=[('ps', bk)])
                if oc < 4:
                    evac_copy(UT[:, oc, 0:T], ps[:, bk, 0:T], r=[('ps', bk)], w=[('UT', oc)])
                else:
                    evac_copy(xbc_o[:, oc - 12, 0:T], ps[:, bk, 0:T], r=[('ps', bk)], w=['xbc_o'])
            if isctx:
                B.st(XBCC[:, :, 2:2 + CTXL], xbc_o[:, :, 0:CTXL], r=['xbc_o'], w=['XBCC'])
            else:
                B.st(XBC[:, :, 2 + blk * 512:2 + (blk + 1) * 512], xbc_o, r=['xbc_o'], w=[('XBC', blk)])
            for t in range(NTI):
                if not isctx:
                    bA = B.nb()
                    bB = B.nb()
                    for g in range(4):
                        B.mm(ps[:, bA, g * 128:(g + 1) * 128], UT[:, g, t * 128:(t + 1) * 128], dftd[:, 0:128], True, True, r=[('UT', g), 'cb16'], w=[('ps', bA)])
                        B.mm(ps[:, bB, g * 128:(g + 1) * 128], UT[:, g, t * 128:(t + 1) * 128], dftd[:, 128:256], True, True, r=[('UT', g), 'cb16'], w=[('ps', bB)])
                    evac_copy(v_o[:, t, 0:512], ps[:, bA, :], r=[('ps', bA)], w=['v_o'])
                    evac_copy(v_o[:, t, 512:1024], ps[:, bB, :], r=[('ps', bB)], w=['v_o'])
                    for half in range(2):
                        bk = B.nb()
                        for kc in range(8):
                            B.mm(ps[:, bk, :], hT[:, kc, t * 128:(t + 1) * 128], win[:, kc, 512 + half * 512:1024 + half * 512], kc == 0, kc == 7, r=WINK + [hk], w=[('ps', bk)])
                        B.act(zs_o[:, t, half * 512:(half + 1) * 512], ps[:, bk, :], AF.Silu, r=[('ps', bk)], w=['zs_o'])
                bk = B.nb()
                for kc in range(8):
                    B.mm(ps[:, bk, 0:32], hT[:, kc, t * 128:(t + 1) * 128], win[:, kc, 3072:3104], kc == 0, kc == 7, r=WINK + [hk], w=[('ps', bk)])
                ti = (32 + t) if isctx else (blk * 4 + t)
                B.tt('dve', dtt, ps[:, bk, 0:32], rbc('dtbias'), ALU.add, r=[('ps', bk), 'rows_bc'], w=['dtt'])
                B.act(dtt2, dtt, AF.Exp, r=['dtt'], w=['dtt2'])
                B.act(dt_all[:, ti, :], dtt2, AF.Ln, r=['dtt2'], w=[('dt_all', ti)], bias=1.0)
            if not isctx:
                B.st(VD[:, blk * 4:(blk + 1) * 4, :], v_o, r=['v_o'], w=[('VD', blk)])
                B.st(ZS[:, blk * 4:(blk + 1) * 4, :], zs_o, r=['zs_o'], w=[('ZS', blk)])

        A_stageA(0)
        for blk in range(9):
            if blk + 1 < 9:
                A_stageA(blk + 1)
            A_stageB(blk)
        B.P.barrier()
        snap("dbg_dt", dt_all[:].rearrange("p a b -> p (a b)"))
        snap("dbg_xbc", XBC)
        snap("dbg_vd", VD)
        snap("dbg_zs", ZS)

        B.arena_reset()
        V = B.carve([32, 1024], BF16)
        tcs = [B.carve([2, 4096], BF16) for _ in range(2)]
        four_all = B.carve([4, L], BF16)
        Qs = B.carve([512], F32)
        ysum = B.carve([512], BF16)
        ydif = B.carve([512], BF16)
        for q in range(4):
            B.ld(V[:, q * 8:(q + 1) * 8, :], VD[:, q * 8:(q + 1) * 8, :], r=[], w=[('V', q)])
        VKEYS = [('V', q) for q in range(4)]
        for kc in range(17):
            tb = tcs[kc % 2]
            tk = ('tcs', kc % 2)
            B.dma('sp' if kc % 2 else 'pool', tb, tcs_d[kc].rearrange("p (a b) -> p a b", a=2), r=[], w=[tk])
            bP = B.nb()
            bQ = B.nb()
            for a in range(32):
                B.mm(ps[:, bP, :], tb[:, 0, a * 128:(a + 1) * 128], V[:, a, 0:512], a == 0, a == 31, r=[tk] + VKEYS, w=[('ps', bP)])
            for a in range(32):
                B.mm(ps[:, bQ, :], tb[:, 1, a * 128:(a + 1) * 128], V[:, a, 512:1024], a == 0, a == 31, r=[tk] + VKEYS, w=[('ps', bQ)])
            B.copy('act', Qs, ps[:, bQ, :], r=[('ps', bQ)], w=['Qs'])
            B.tt('dve', ysum, ps[:, bP, :], Qs, ALU.add, r=[('ps', bP), 'Qs'], w=['ysum'])
            bt = B.nb()
            ptv = ps[:, bt, 0:256].bitcast(BF16)
            for g in range(4):
                B.tr(ptv[:, g * 128:(g + 1) * 128], ysum[:, g * 128:(g + 1) * 128], identb, r=['ysum', 'cb16'], w=[('ps', bt)])
            pt3 = ptv.rearrange("p (g i) -> p g i", g=4)
            if kc < 16:
                B.tt('dve', ydif, ps[:, bP, :], Qs, ALU.subtract, r=[('ps', bP), 'Qs'], w=['ydif'])
                B.copy('act', four_all[:, :, kc * 128:(kc + 1) * 128], pt3, r=[('ps', bt)], w=['four_all'])
                bt2 = B.nb()
                ptv2 = ps[:, bt2, 0:256].bitcast(BF16)
                for g in range(4):
                    B.tr(ptv2[:, g * 128:(g + 1) * 128], ydif[:, g * 128:(g + 1) * 128], identb, r=['ydif', 'cb16'], w=[('ps', bt2)])
                pt32 = ptv2.rearrange("p (g i) -> p g i", g=4)
                base = (31 - kc) * 128
                B.copy('dve', four_all[:, :, base + 127:base:-1], pt32[:, :, 1:128], r=[('ps', bt2)], w=['four_all'])
                if kc >= 1:
                    B.copy('dve', four_all[:, :, (32 - kc) * 128:(32 - kc) * 128 + 1], pt32[:, :, 0:1], r=[('ps', bt2)], w=['four_all'])
            else:
                B.copy('act', four_all[:, :, 2048:2049], pt3[:, :, 0:1], r=[('ps', bt)], w=['four_all'])
        for q in range(4):
            B.st(MIXT[:, 0:4, q * 1024:(q + 1) * 1024], four_all[:, :, q * 1024:(q + 1) * 1024], r=['four_all'], w=[('MIXTf', q)])
        B.P.barrier()

        B.arena_reset()
        B.ldq, B.stq = 'sp', 'pool'
        adab1 = [B.carve([8, 512], BF16) for _ in range(2)]
        Dg = B.carve([60, 128], BF16)
        xws = [B.carve([12, 516], BF16) for _ in range(2)]
        BCTs = [B.carve([4, 512], BF16) for _ in range(2)]
        xsbs = [B.carve([1280], BF16) for _ in range(2)]
        dtAs = [B.carve([16], BF16) for _ in range(2)]
        ectes = [B.carve([48], F32) for _ in range(2)]
        Rf = B.carve([2048], BF16)
        lndt = B.carve([16], F32)
        wtes = [B.carve([16], F32) for _ in range(2)]
        Decf = B.carve([2048], BF16)
        Wtfs = [B.carve([2048], BF16) for _ in range(2)]
        CBm = B.carve([2, 128], BF16)
        xdts = [B.carve([1024], BF16) for _ in range(2)]
        xdtte = B.carve([1024], BF16)
        hst = [B.carve([1024], F32) for _ in range(2)]
        hbf = B.carve([1024], BF16)
        t1s = [B.carve([1024], F32) for _ in range(2)]
        yaccs = [B.carve([1024], F32) for _ in range(2)]
        zss = [B.carve([1024], BF16) for _ in range(2)]
        junk = B.carve([512], BF16)
        ssq = B.carve([2], F32)
        rs2 = B.carve([2], F32)
        yns = [B.carve([1024], BF16) for _ in range(2)]
        ymix_sts = [B.carve([8, 512], BF16) for _ in range(2)]
        convb_row32 = B.carve([1536], F32)
        convb_row = B.carve([1536], BF16)
        XSB = B.dscr("XSB", [128, 32, 1280], BF16)
        BCTD = B.dscr("BCTD", [128, 8, 2048], BF16)
        B.ld(convb_row32[0:1, :], rows_d[0:1, ROWS['convb'][0]:ROWS['convb'][0] + 1536], r=[], w=['convb32'])
        B.copy('dve', convb_row[0:1, :], convb_row32[0:1, :], r=['convb32'], w=['convb_row'])
        o_cw, _ = COLS['convw']
        for k in range(5):
            for cc in range(12):
                B.ts('dve', Dg[:, k * 12 + cc, :], identb, cols[:, o_cw + k * 12 + cc:o_cw + k * 12 + cc + 1], None, ALU.mult, None,
                     r=['cb16', 'cols'], w=[('Dg', k, cc)])
        DGK = [('Dg', k, cc) for k in range(5) for cc in range(12)]
        B.memset('dve', hst[0], 0.0, w=[('hst', 0)])
        B.memset('dve', hst[1], 0.0, w=[('hst', 1)])
        ones_row = onesb[0:1, :]

        items = []
        for d in range(2):
            for c in ([0, 1] if d == 0 else [1, 0]):
                items.append(dict(d=d, ti=32 + c, ctx=True, blk=-1 - d, c=c, want_y=False, first=(c == (0 if d == 0 else 1)), last=False, ci=None))
        for d in range(2):
            blks = range(8) if d == 0 else range(7, -1, -1)
            for blk in blks:
                cs = list(range(4)) if d == 0 else list(range(3, -1, -1))
                for n_, c in enumerate(cs):
                    items.append(dict(d=d, ti=blk * 4 + c, ctx=False, blk=blk, c=c, want_y=True, first=(n_ == 0), last=(n_ == 3), ci=blk * 4 + c))
        bseq = -1
        for i, it in enumerate(items):
            if it['first']:
                bseq += 1
            it['bp'] = bseq % 2
            it['par'] = i % 2
            it['prev_item'] = items[i - 1] if i >= 1 else None
            it['start_dir'] = (i == 4) or (i == 4 + 32)

        def ctxv(it):
            d, ti, c, par, bp = it['d'], it['ti'], it['c'], it['par'], it['bp']
            v = dict(d=d, ti=ti, c=c, par=par, bp=bp)
            v['xw'] = xws[bp]; v['xwk'] = ('xw', bp)
            v['BCT'] = BCTs[bp]; v['bctk'] = ('BCT', bp)
            v['xsb'] = xsbs[par]; v['xs_tm'] = xsbs[par][:, 0:1024]; v['B_tm'] = xsbs[par][:, 1024:1280]; v['xsk'] = ('xsb', par)
            v['dtA'] = dtAs[par]; v['ecte'] = ectes[par]; v['ek'] = ('ecte', par)
            v['Mincl'] = masks[:, 0 + 2 * d, :]; v['Mstrict'] = masks[:, 1 + 2 * d, :]
            v['dt_c'] = dt_all[:, ti, d * 16:(d + 1) * 16]
            v['xdt'] = xdts[par]; v['t1'] = t1s[par]
            v['hs'] = hst[d]; v['hk'] = ('hst', d)
            v['T1K'] = [('t1', par, 0), ('t1', par, 1)]
            return v

        def S1a(it):
            v = ctxv(it)
            d, ti, c, par, bp = v['d'], v['ti'], v['c'], v['par'], v['bp']
            xw, xwk, BCT, bctk, xsb, xs_tm, B_tm, xsk = v['xw'], v['xwk'], v['BCT'], v['bctk'], v['xsb'], v['xs_tm'], v['B_tm'], v['xsk']
            dtA, ecte, Mincl, Mstrict, dt_c = v['dtA'], v['ecte'], v['Mincl'], v['Mstrict'], v['dt_c']
            c_off = c * 128
            if it['first']:
                if it['ctx']:
                    B.ld(xw[:, :, 0:CTXL + 4], XBCC, r=[], w=[xwk])
                elif d == 0:
                    blk = it['blk']
                    B.ld(xw, XBC[:, :, blk * 512:blk * 512 + 516], r=[], w=[xwk])
                    for cc in range(8, 12):
                        bk = B.nb()
                        for k in range(5):
                            B.mm(ps[:, bk, :], Dg[:, k * 12 + cc, :], xw[:, cc, k:k + 512], k == 0, k == 4, r=[xwk] + DGK, w=[('ps', bk)])
                        B.act(BCT[:, cc - 8, :], ps[:, bk, :], AF.Silu, r=[('ps', bk)], w=[bctk], bias=col('convb', cc))
                else:
                    blk = it['blk']
                    B.ld(BCT.rearrange("p a b -> p (a b)"), BCTD[:, blk, :], r=[('BCTD', blk)], w=[bctk])
            if it['ctx'] or d == 0:
                for grp in range(3):
                    ccs = [0, 1, 2, 3] if grp == 0 else ([4, 5, 6, 7] if grp == 1 else [8, 9])
                    bk = B.nb()
                    for cc in ccs:
                        dst = ps[:, bk, (cc % 4) * 128:(cc % 4 + 1) * 128]
                        for k in range(5):
                            B.mm(dst, xw[:, cc, c_off + k:c_off + k + 128], Dg[:, k * 12 + cc, :], k == 0, False, r=[xwk] + DGK, w=[('ps', bk)])
                        B.mm(dst, ones_row, convb_row[0:1, cc * 128:(cc + 1) * 128], False, True, r=['cb16', 'convb_row'], w=[('ps', bk)])
                    if grp < 2:
                        B.act(xs_tm[:, grp * 512:(grp + 1) * 512], ps[:, bk, :], AF.Silu, r=[('ps', bk)], w=[xsk])
                    else:
                        B.act(B_tm, ps[:, bk, 0:256], AF.Silu, r=[('ps', bk)], w=[xsk])
            else:
                B.ld(xsb, XSB[:, it['ci'], :], r=[('XSB', it['ci'])], w=[xsk])
                B.ld(zss[par], ZS[:, it['ci'], :], r=[], w=[('zs', par)])
            B.tt('dve', dtA, dt_c, a_bc[:, d * 16:(d + 1) * 16], ALU.mult, r=[('dt_all', ti), 'a_bc'], w=[('dtA', par)])
            bk = B.nb()
            B.mm(ps[:, bk, 0:16], Mincl, dtA, True, True, r=[('dtA', par), 'cb16'], w=[('ps', bk)])
            B.mm(ps[:, bk, 16:32], Mstrict, dtA, True, True, r=[('dtA', par), 'cb16'], w=[('ps', bk)])
            B.mm(ps[:, bk, 32:48], onesb, dtA, True, True, r=[('dtA', par), 'cb16'], w=[('ps', bk)])
            B.act(ecte, ps[:, bk, 0:48], AF.Exp, r=[('ps', bk)], w=[('ecte', par)])
            if it['want_y']:
                R3 = Rf.rearrange("p (h i) -> p h i", h=16)
                B.tt('dve', R3[:, 0:8, :], Mincl.unsqueeze(1).to_broadcast([128, 8, 128]),
                     dtA[:, 0:8].unsqueeze(2).to_broadcast([128, 8, 128]), ALU.mult, r=[('dtA', par), 'cb16'], w=[('R', 0)])
                B.tt('pool', R3[:, 8:16, :], Mincl.unsqueeze(1).to_broadcast([128, 8, 128]),
                     dtA[:, 8:16].unsqueeze(2).to_broadcast([128, 8, 128]), ALU.mult, r=[('dtA', par), 'cb16'], w=[('R', 1)])
                B.act(lndt, dt_c, AF.Ln, r=[('dt_all', ti)], w=['lndt'])
                for q in range(4):
                    b = B.nb()
                    B.mm(ps[:, b, :], Mstrict, Rf[:, 512 * q:512 * (q + 1)], True, True, r=[('R', q // 2), 'cb16'], w=[('ps', b)])
                    for hh in range(4):
                        h_ = 4 * q + hh
                        B.act(Decf[:, 128 * h_:128 * (h_ + 1)], ps[:, b, 128 * hh:128 * (hh + 1)], AF.Exp, r=[('ps', b), 'lndt'], w=[('Dec', h_)],
                              bias=lndt[:, h_:h_ + 1])
                bcb = B.nb()
                it['bcb'] = bcb
                for g in range(2):
                    B.mm(ps[:, bcb, g * 128:(g + 1) * 128], BCT[:, g, c * 128:(c + 1) * 128], BCT[:, 2 + g, c * 128:(c + 1) * 128], True, True,
                         r=[bctk], w=[('ps', bcb)])
                B.tt('dve', CBm, ps[:, bcb, 0:256].rearrange("p (g i) -> p g i", g=2), Mincl.unsqueeze(1).to_broadcast([128, 2, 128]),
                     ALU.mult, r=[('ps', bcb), 'cb16'], w=['CBm'])

        def S1b(it):
            v = ctxv(it)
            d, ti, par = v['d'], v['ti'], v['par']
            B.tt('dve', wtes[par], v['dt_c'], v['ecte'][:, 16:32], ALU.mult, r=[('dt_all', ti), v['ek']], w=[('wte', par)])
            if it['want_y']:
                if d == 0:
                    B.tt('pool', yaccs[par].rearrange("p (h q) -> p h q", h=16), v['xs_tm'].rearrange("p (h q) -> p h q", h=16),
                         rbc('dskip').unsqueeze(2).to_broadcast([128, 16, 64]), ALU.mult, r=[v['xsk'], 'rows_bc'], w=[('yacc', par)])
                    B.st(XSB[:, it['ci'], :], v['xsb'], r=[v['xsk']], w=[('XSB', it['ci'])])
                    if it['first']:
                        B.st(BCTD[:, it['blk'], :], v['BCT'].rearrange("p a b -> p (a b)"), r=[v['bctk']], w=[('BCTD', it['blk'])])

        def S1c(it):
            if not it['want_y']:
                return
            par = it['par']
            Wt3 = Wtfs[par].rearrange("p (h i) -> p h i", h=16)
            Dec3 = Decf.rearrange("p (h i) -> p h i", h=16)
            for g in range(2):
                B.tt('dve', Wt3[:, 8 * g:8 * g + 8, :], Dec3[:, 8 * g:8 * g + 8, :], CBm[:, g, :].unsqueeze(1).to_broadcast([128, 8, 128]),
                     ALU.mult, r=[('Dec', 8 * g + j_) for j_ in range(8)] + ['CBm'], w=[('Wt', par, g)])

        def S2a(it):
            v = ctxv(it)
            d, c, par = v['d'], v['c'], v['par']
            BCT, bctk, B_tm, xsk, ecte, ek, xdt, hs, hk = v['BCT'], v['bctk'], v['B_tm'], v['xsk'], v['ecte'], v['ek'], v['xdt'], v['hs'], v['hk']
            if it['start_dir']:
                B.copy('act', hbf, hs, r=[hk], w=['hbf'])
            pv = it.get('prev_item')
            if pv is not None and pv['want_y'] and pv['d'] == 0:
                ppar = pv['par']
                B.st(YF[:, pv['ci'], :], t1s[ppar], r=[('t1', ppar, 0), ('t1', ppar, 1)], w=[('YF', pv['ci'])])
            if it['want_y'] and d == 1:
                B.ld(yaccs[par], YF[:, it['ci'], :], r=[('YF', it['ci'])], w=[('yacc', par)])
            B.tt('pool', xdtte.rearrange("p (h q) -> p h q", h=16), v['xs_tm'].rearrange("p (h q) -> p h q", h=16),
                 wtes[par].unsqueeze(2).to_broadcast([128, 16, 64]), ALU.mult, r=[xsk, ('wte', par)], w=['xdtte'])
            if it['want_y']:
                ob0 = B.nb2()
                ob = [ob0, ob0 + 1]
                for g in range(2):
                    B.mm(ps[:, ob[g], :], BCT[:, 2 + g, c * 128:(c + 1) * 128], hbf[:, g * 512:(g + 1) * 512], True, True,
                         r=[bctk, 'hbf'], w=[('ps', ob[g])])
                it['ob'] = ob
            sb0 = B.nb2()
            sbk = [sb0, sb0 + 1]
            for g in range(2):
                sl = slice(g * 512, (g + 1) * 512)
                B.mm(ps[:, sbk[g], :], B_tm[:, g * 128:(g + 1) * 128], xdtte[:, sl], True, True, r=[xsk, 'xdtte'], w=[('ps', sbk[g])])
            it['sbk'] = sbk
            B.tt('dve', hs.rearrange("p (h q) -> p h q", h=16), hs.rearrange("p (h q) -> p h q", h=16),
                 ecte[:, 32:48].unsqueeze(2).to_broadcast([128, 16, 64]), ALU.mult, r=[hk, ek], w=[hk])
            B.tt('dve', hs, hs, ps[:, sb0:sb0 + 2, :].rearrange("p a n -> p (a n)"), ALU.add, r=[hk, ('ps', sbk[0]), ('ps', sbk[1])], w=[hk])
            B.copy('act', hbf, hs, r=[hk], w=['hbf'])
            if it['want_y']:
                Wt3 = Wtfs[par].rearrange("p (h i) -> p h i", h=16)
                yb0 = B.nb2()
                yb = [yb0, yb0 + 1]
                for h in range(16):
                    B.mm(ps[:, yb[h // 8], (h % 8) * 64:(h % 8 + 1) * 64], Wt3[:, h, :], v['xs_tm'][:, h * 64:(h + 1) * 64], True, True,
                         r=[('Wt', par, h // 8), xsk], w=[('ps', yb[h // 8])])
                it['yb'] = yb

        def S2b(it):
            v = ctxv(it)
            d, par = v['d'], v['par']
            ecte, ek, hs, hk, t1, T1K = v['ecte'], v['ek'], v['hs'], v['hk'], v['t1'], v['T1K']
            if it['want_y']:
                yb, ob = it['yb'], it['ob']
                B.tt('dve', t1.rearrange("p (h q) -> p h q", h=16), ps[:, ob[0]:ob[0] + 2, :].rearrange("p a (h q) -> p (a h) q", q=64),
                     ecte[:, 0:16].unsqueeze(2).to_broadcast([128, 16, 64]), ALU.mult, r=[('ps', ob[0]), ('ps', ob[1]), ek], w=T1K)
                B.tt('dve', t1, t1, ps[:, yb[0]:yb[0] + 2, :].rearrange("p a n -> p (a n)"), ALU.add, r=T1K + [('ps', yb[0]), ('ps', yb[1])], w=T1K)
            if it['want_y']:
                B.tt('dve', t1, t1, yaccs[par], ALU.add, r=T1K + [('yacc', par)], w=T1K)

        def S2c(it):
            if not (it['want_y'] and it['d'] == 1):
                return
            v = ctxv(it)
            par, t1, T1K = v['par'], v['t1'], v['T1K']
            B.tt('dve', t1, t1, zss[par], ALU.mult, r=T1K + [('zs', par)], w=T1K)
            for g in range(2):
                B.act(junk, t1[:, g * 512:(g + 1) * 512], AF.Square, r=T1K, w=['junk', ('ssq', g)], accum_out=ssq[:, g:g + 1])
            B.act(rs2, ssq, AF.Ln, r=[('ssq', 0), ('ssq', 1)], w=['rs2a'], scale=1.0 / 512, bias=EPS)
            B.act(rs2, rs2, AF.Exp, r=['rs2a'], w=['rs2'], scale=-0.5)
            for g in range(2):
                B.act(yns[par][:, g * 512:(g + 1) * 512], t1[:, g * 512:(g + 1) * 512], AF.Copy, r=T1K + ['rs2'], w=[('yn', par, g)], scale=rs2[:, g:g + 1])

        def S2t(it):
            if not (it['want_y'] and it['d'] == 1):
                return
            par, bp, c = it['par'], it['bp'], it['c']
            ymix_st = ymix_sts[bp]
            for half in range(2):
                bt = B.nb()
                ptv = ps[:, bt, 0:256].bitcast(BF16)
                for j in range(4):
                    fc = half * 4 + j
                    B.tr(ptv[:, j * 128:(j + 1) * 128], yns[par][:, fc * 128:(fc + 1) * 128], identb, r=[('yn', par, half), 'cb16'], w=[('ps', bt)])
                B.copy('act', ymix_st[:, half * 4:half * 4 + 4, c * 128:(c + 1) * 128], ptv.rearrange("p (g i) -> p g i", g=4),
                       r=[('ps', bt)], w=[('ymix_st', bp)])
            if it['last']:
                blk = it['blk']
                B.st(MIXT[:, 4:12, blk * 512:(blk + 1) * 512], ymix_st, r=[('ymix_st', bp)], w=[('MIXTy', blk)])

        NI = len(items)
        cj = list(cast_jobs0)
        cj1s = list(cast_jobs1)
        ada_next = 0
        for i in range(NI + 2):
            nxt = items[i] if i < NI else None
            cur = items[i - 1] if 1 <= i <= NI else None
            prv = items[i - 2] if 2 <= i <= NI + 1 else None
            if nxt is not None:
                S1a(nxt)
            if cur is not None:
                S2a(cur)
            if prv is not None:
                S2t(prv)
            if nxt is not None:
                S1b(nxt)
            if cur is not None:
                S2b(cur)
            if nxt is not None:
                S1c(nxt)
            if cur is not None:
                S2c(cur)
            if i >= 4 and i % 5 == 4 and ada_next < 12:
                if ada_next == 0:
                    adaln_load(1, 0, adab1)
                else:
                    adaln_mm(1, ada_next - 1, adab1)
                    if ada_next < 12:
                        adaln_load(1, ada_next, adab1)
                ada_next += 1
            if i >= 6 and i % 8 == 6 and cj:
                cast(*cj.pop(0))
            if i >= 42 and i % 3 == 0 and cj1s:
                cast(*cj1s.pop(0))
        while ada_next < 12:
            if ada_next >= 1:
                adaln_mm(1, ada_next - 1, adab1)
            adaln_load(1, ada_next, adab1)
            ada_next += 1
        adaln_mm(1, 11, adab1)
        adaln_finish(1)
        for j in cj + cj1s:
            cast(*j)
        B.ldq, B.stq = 'pool', 'sp'
        B.P.barrier()

        def outproj(layer, woutb, gfold):
            B.arena_reset()
            wo = B.carve([12, 1024], BF16)
            mixb = [B.carve([12, 512], BF16) for _ in range(2)]
            lb = [B.carve([8, 512], F32) for _ in range(2)]
            B.dma('sp', wo, woutb.rearrange("p (c n) -> p c n", c=12), r=['w_out0b' if layer == 0 else 'w_out1b'], w=['wo'])
            if gfold:
                for fc in range(8):
                    B.ts('dve', wo[:, 4 + fc, :], wo[:, 4 + fc, :], col('ssdg', fc), None, ALU.mult, None, r=['wo', 'cols'], w=['wo'])
            for blk in range(8):
                mb = mixb[blk % 2]
                mk = ('mixb', blk % 2)
                l_ = lb[blk % 2]
                lk = ('lb', blk % 2)
                B.ld(mb, MIXT[:, :, blk * 512:(blk + 1) * 512], r=[], w=[mk])
                B.ld(l_, LAT[:, :, blk * 512:(blk + 1) * 512], r=[], w=[lk])
                for oc in range(8):
                    bk = B.nb()
                    for kc in range(12):
                        B.mm(ps[:, bk, :], wo[:, kc, oc * 128:(oc + 1) * 128], mb[:, kc, :], kc == 0, kc == 11, r=['wo', mk], w=[('ps', bk)])
                    B.stt(l_[:, oc, :], ps[:, bk, :], modcol(layer, 2, 0, oc), l_[:, oc, :], ALU.mult, ALU.add, r=[('ps', bk), lk] + VK, w=[lk])
                B.st(LAT[:, :, blk * 512:(blk + 1) * 512], l_, r=[lk], w=[('LATo', blk)])
            B.P.barrier()
            snap("dbg_mixt%d" % layer, MIXT)
            snap("dbg_lat%d" % (1 if layer == 0 else 3), LAT)
            B.P.barrier()

        def ffn(layer, final):
            B.arena_reset()
            W1 = B.carve([8, 4096], BF16)
            W2 = B.carve([32, 1024], BF16)
            TB = 256
            NBK = L // TB
            lbks = [B.carve([8, TB], F32) for _ in range(2)]
            hfs = [B.carve([8, TB], BF16) for _ in range(2)]
            sqf = B.carve([8, TB], BF16)
            rstdf = B.carve([TB], F32)
            lnvf = B.carve([TB], F32)
            tmpn = [B.carve([TB], F32) for _ in range(2)]
            tmpr = [B.carve([TB], F32) for _ in range(2)]
            uflat = B.carve([32 * TB], BF16)
            u = uflat.rearrange("p (a b) -> p a b", a=32)
            otm = B.carve([2, 1024], F32) if final else None
            OTMK = ['otm']
            for h in range(2):
                B.dma('sp', W1[:, h * 4:(h + 1) * 4, :], w1b[layer].rearrange("p (c n) -> p c n", c=8)[:, h * 4:(h + 1) * 4, :], r=[('w1b', layer, 0), ('w1b', layer, 1)], w=[('W1', h)])
            for h in range(4):
                B.dma('sp', W2[:, h * 8:(h + 1) * 8, :], w2b[layer].rearrange("p (c n) -> p c n", c=32)[:, h * 8:(h + 1) * 8, :], r=[('w2b', layer, 0), ('w2b', layer, 1)], w=[('W2', h)])
            W1K = [('W1', 0), ('W1', 1)]
            W2K = [('W2', h) for h in range(4)]
            Abase = 8 if layer == 0 else 24

            def stageA1(blk):
                par = blk % 2
                tsl = slice(blk * TB, (blk + 1) * TB)
                B.ld(lbks[par], LAT[:, :, tsl], r=[], w=[('lbk', par)])
                norm_mod1(lbks[par], ('lbk', par), TB, sqf)

            def stageA2(blk):
                par = blk % 2
                norm_mod2(lbks[par], ('lbk', par), TB, Abase, lambda fc: modcol(layer, 3, 0, fc), hfs[par], ('hf', par), sqf, rstdf, lnvf, tmpn)

            def stageB(blk, nxt):
                par = blk % 2
                lbk = lbks[par]
                lk = ('lbk', par)
                hf = hfs[par]
                hk = ('hf', par)
                tsl = slice(blk * TB, (blk + 1) * TB)
                for fc in range(32):
                    bk = B.nb()
                    for kc in range(8):
                        B.mm(ps[:, bk, 0:TB], W1[:, kc, fc * 128:(fc + 1) * 128], hf[:, kc, :], kc == 0, kc == 7, r=W1K + [hk], w=[('ps', bk)])
                    B.act(tmpr[fc % 2], ps[:, bk, 0:TB], AF.Relu, r=[('ps', bk)], w=[('rl', fc % 2)])
                    B.tt('dve', u[:, fc, :], tmpr[fc % 2], tmpr[fc % 2], ALU.mult, r=[('rl', fc % 2)], w=[('u', fc)])
                    if final and blk >= 1 and fc == 10:
                        stageF1(blk - 1)
                UK = [('u', fc) for fc in range(32)]
                if final and blk >= 1:
                    stageF2(blk - 1)
                if nxt is not None:
                    stageA1(nxt)
                for oc in range(8):
                    if oc == 3 and nxt is not None:
                        stageA2(nxt)
                    bk = B.nb()
                    for fc in range(32):
                        B.mm(ps[:, bk, 0:TB], W2[:, fc, oc * 128:(oc + 1) * 128], u[:, fc, :], fc == 0, fc == 31, r=W2K + [('u', fc)], w=[('ps', bk)])
                    B.stt(lbk[:, oc, :], ps[:, bk, 0:TB], modcol(layer, 5, 0, oc), lbk[:, oc, :], ALU.mult, ALU.add, r=[('ps', bk), lk], w=[lk])
                if not final:
                    B.st(LAT[:, :, tsl], lbk, r=[lk], w=[('LATf', blk)])

            def stageF1(blk):
                par = blk % 2
                lbk = lbks[par]
                lk = ('lbk', par)
                if True:
                    for fc in range(8):
                        B.act(sqf[:, fc, :], lbk[:, fc, :], AF.Square, r=[lk], w=[('sq', fc)])
                    bk = B.nb()
                    for fc in range(8):
                        B.mm(ps[:, bk, 0:TB], onesb, sqf[:, fc, :], fc == 0, fc == 7, r=[('sq', fc), 'cb16'], w=[('ps', bk)])
                    B.act(lnvf, ps[:, bk, 0:TB], AF.Ln, r=[('ps', bk)], w=['lnv'], scale=1.0 / DM, bias=EPS)
                    B.act(rstdf, lnvf, AF.Exp, r=['lnv'], w=['rstd'], scale=-0.5)
                    for fc in range(8):
                        B.stt(lbk[:, fc, :], lbk[:, fc, :], col('fing', fc), rstdf, ALU.mult, ALU.mult, r=[lk, 'rstd', 'cols'], w=[lk])

            def stageF2(blk):
                par = blk % 2
                lbk = lbks[par]
                lk = ('lbk', par)
                if True:
                    for t in range(TB // 128):
                        for half in range(2):
                            bk = B.nb()
                            for j in range(4):
                                fc = half * 4 + j
                                B.tr(ps[:, bk, j * 128:(j + 1) * 128], lbk[:, fc, t * 128:(t + 1) * 128], ident[:], r=[lk, 'ident'], w=[('ps', bk)])
                            evac_copy(otm[:, t, half * 512:(half + 1) * 512], ps[:, bk, :], r=[('ps', bk)] + OTMK, w=OTMK)
                    B.st(out.rearrange("(t p) d -> p t d", p=128)[:, blk * 2:(blk + 1) * 2, :], otm, r=OTMK, w=[('out', blk)])

            cj1 = []
            stageA1(0)
            stageA2(0)
            for blk in range(NBK):
                stageB(blk, blk + 1 if blk + 1 < NBK else None)
            if final:
                stageF1(NBK - 1)
                stageF2(NBK - 1)
                if cj1:
                    cast(*cj1.pop(0))
            for j in cj1:
                cast(*j)
            if final:
                B.P.add('sp', lambda e: e.nop(), r=[('out', blk) for blk in range(NBK)])
            B.P.barrier()
            if not final:
                snap("dbg_lat2", LAT)
                B.P.barrier()

        outproj(0, w_out0b, True)
        ffn(0, False)

        B.arena_reset()
        B.ldq, B.stq = 'pool', 'sp'
        win1 = B.carve([8, 2560], BF16)
        wsT = B.carve([4, 128], BF16)
        pw = B.carve([4, 128], BF16)
        bias2 = B.carve([8, 128], F32)
        latbs = [B.carve([8, 512], F32) for _ in range(2)]
        sq1 = B.carve([8, 512], BF16)
        rstd1 = B.carve([512], F32)
        lnv1 = B.carve([512], F32)
        tmp1 = [B.carve([512], F32) for _ in range(2)]
        h1s = [B.carve([8, 512], BF16) for _ in range(2)]
        uT = B.carve([8, 512], BF16)
        vgs = [B.carve([1024], F32) for _ in range(2)]
        vhats = [B.carve([1024], BF16) for _ in range(2)]
        bsts = [B.carve([2, 6], F32) for _ in range(2)]
        mvs = [B.carve([2], F32) for _ in range(2)]
        rsvs = [B.carve([1], F32) for _ in range(2)]
        p_all = B.carve([32, 512], BF16)
        vm = B.carve([512], F32)
        sgu_st = B.carve([8, 512], BF16)
        pl = B.carve([4, 128], BF16)
        pool_st = B.carve([4, 512], BF16)
        B.dma('sp', win1, w_in1b.rearrange("p (c n) -> p c n", c=8), r=['w_in1b'], w=['win1'])
        B.dma('sp', wsT, w_sTb.rearrange("p (g i) -> p g i", g=4), r=['w_sTb'], w=['wsT'])
        B.dma('sp', pw, pool_wb.rearrange("p (g i) -> p g i", g=4), r=['pool_wb'], w=['pw'])
        for g in range(4):
            bk = B.nb()
            B.mm(ps[:, bk, 0:128], onesb, wsT[:, g, :], True, True, r=['wsT', 'cb16'], w=[('ps', bk)])
            o_bs, _ = ROWS['bs']
            for j in range(2):
                fc = g * 2 + j
                B.stt(bias2[:, fc, :], ps[:, bk, 0:128], col('lnb', fc), rows_bc[:, o_bs + g * 128:o_bs + (g + 1) * 128], ALU.mult, ALU.add,
                      r=[('ps', bk), 'cols', 'rows_bc'], w=['bias2'])

        def E_stageA(blk):
            par = blk % 2
            B.ld(latbs[par], LAT[:, :, blk * 512:(blk + 1) * 512], r=[], w=[('latb', par)])
            norm_mod(latbs[par], ('latb', par), 512, 16, lambda fc: modcol(1, 0, 0, fc), h1s[par], ('h1', par), sq1, rstd1, lnv1, tmp1)

        def E_VP(blk, t):
            h1 = h1s[blk % 2]
            hk = ('h1', blk % 2)
            vp = t % 2
            vg = vgs[vp]
            ti = blk * 4 + t
            for half in range(2):
                bk = B.nb()
                for kc in range(8):
                    B.mm(ps[:, bk, :], h1[:, kc, t * 128:(t + 1) * 128], win1[:, kc, 1024 + half * 512:1536 + half * 512], kc == 0, kc == 7,
                         r=['win1', hk], w=[('ps', bk)])
                B.act(vg[:, half * 512:(half + 1) * 512], ps[:, bk, :], AF.Gelu_apprx_tanh, r=[('ps', bk)], w=[('vg', vp, half)])
                B.P.add('dve', (lambda hh, vv, bb: (lambda e: e.bn_stats(out=bb[:, hh, :], in_=vv[:, hh * 512:(hh + 1) * 512])))(half, vg, bsts[vp]),
                        r=[('vg', vp, half)], w=[('bst', vp, half)])
            B.P.add('dve', (lambda mm_, bb: (lambda e: e.bn_aggr(out=mm_, in_=bb)))(mvs[vp], bsts[vp]), r=[('bst', vp, 0), ('bst', vp, 1)], w=[('mv', vp)])
            B.act(rsvs[vp], mvs[vp][:, 1:2], AF.Ln, r=[('mv', vp)], w=[('rsva', vp)], bias=EPS)
            B.act(rsvs[vp], rsvs[vp], AF.Exp, r=[('rsva', vp)], w=[('rsv', vp)], scale=-0.5)
            VGK = [('vg', vp, 0), ('vg', vp, 1)]
            B.ts('dve', vg, vg, mvs[vp][:, 0:1], rsvs[vp][:, 0:1], ALU.subtract, ALU.mult, r=VGK + [('mv', vp), ('rsv', vp)], w=VGK)
            B.tt('dve', vhats[vp], vg, rbc('lng'), ALU.mult, r=VGK + ['rows_bc'], w=[('vhat', vp)])

        def E_PP(blk, t):
            h1 = h1s[blk % 2]
            hk = ('h1', blk % 2)
            ti = blk * 4 + t
            bk = B.nb()
            for kc in range(8):
                B.mm(ps[:, bk, :], h1[:, kc, t * 128:(t + 1) * 128], win1[:, kc, 2048:2560], kc == 0, kc == 7, r=['win1', hk], w=[('ps', bk)])
            evac_copy(p_all[:, ti, :], ps[:, bk, :], r=[('ps', bk)], w=[('p_all', ti)])

        def E_SP(blk, t):
            vp = t % 2
            vhat = vhats[vp]
            for half in range(2):
                bk = B.nb()
                for j in range(4):
                    fc = half * 4 + j
                    B.mm(ps[:, bk, j * 128:(j + 1) * 128], vhat[:, fc * 128:(fc + 1) * 128], wsT[:, fc // 2, :], True, True, r=[('vhat', vp), 'wsT'], w=[('ps', bk)])
                B.tt('dve', vm.rearrange("p (a b) -> p a b", a=4), ps[:, bk, :].rearrange("p (a b) -> p a b", a=4), bias2[:, half * 4:half * 4 + 4, :],
                     ALU.add, r=[('ps', bk), 'bias2'], w=['vm'])
                B.tt('dve', sgu_st[:, half * 4:half * 4 + 4, t * 128:(t + 1) * 128], vm.rearrange("p (a b) -> p a b", a=4),
                     uT[:, half * 4:half * 4 + 4, t * 128:(t + 1) * 128], ALU.mult, r=['vm'] + [('uT', half * 4 + j) for j in range(4)], w=['sgu_st'])

        def E_stageB(blk):
            h1 = h1s[blk % 2]
            hk = ('h1', blk % 2)
            E_VP(blk, 0)
            for oc in range(8):
                bk = B.nb()
                for kc in range(8):
                    B.mm(ps[:, bk, :], win1[:, kc, oc * 128:(oc + 1) * 128], h1[:, kc, :], kc == 0, kc == 7, r=['win1', hk], w=[('ps', bk)])
                B.act(uT[:, oc, :], ps[:, bk, :], AF.Gelu_apprx_tanh, r=[('ps', bk)], w=[('uT', oc)])
            for t in range(1, 5):
                if t < 4:
                    E_VP(blk, t)
                else:
                    for tt_ in range(4):
                        E_PP(blk, tt_)
                E_SP(blk, t - 1)
            B.st(MIXT[:, 0:8, blk * 512:(blk + 1) * 512], sgu_st, r=['sgu_st'], w=[('MIXTs', blk)])

        E_stageA(0)
        for blk in range(8):
            if blk + 1 < 8:
                E_stageA(blk + 1)
            E_stageB(blk)
        pls = [pl, B.carve([4, 128], BF16)]

        def pool_band(ti):
            bk = B.nb()
            for g in range(4):
                terms = []
                if ti > 0:
                    terms.append((ti - 1, 1))
                terms.append((ti, 3 if ti == 0 else (4 if ti == 31 else 0)))
                if ti < 31:
                    terms.append((ti + 1, 2))
                for n_, (jt, var) in enumerate(terms):
                    B.mm(ps[:, bk, g * 128:(g + 1) * 128], p_all[:, jt, g * 128:(g + 1) * 128], poolband[:, g, var, :], n_ == 0, n_ == len(terms) - 1,
                         r=[('p_all', jt), 'cb16'], w=[('ps', bk)])
            B.copy('dve', pls[ti % 2], ps[:, bk, :].rearrange("p (g i) -> p g i", g=4), r=[('ps', bk)], w=[('pl', ti % 2)])

        def pool_w_(ti):
            bk2 = B.nb()
            for g in range(4):
                B.mm(ps[:, bk2, g * 128:(g + 1) * 128], pw[:, g, :], pls[ti % 2][:, g, :], True, True, r=['pw', ('pl', ti % 2)], w=[('ps', bk2)])
            for g in range(4):
                dst = pool_st[:, g, (ti % 4) * 128:(ti % 4 + 1) * 128]
                B.act(dst, ps[:, bk2, g * 128:(g + 1) * 128], AF.Copy, r=[('ps', bk2), 'cols'], w=[('pool_st', g)], scale=col('pscale', g))
            if ti % 4 == 3:
                blk = ti // 4
                B.st(MIXT[:, 8:12, blk * 512:(blk + 1) * 512], pool_st, r=[('pool_st', g_) for g_ in range(4)], w=[('MIXTp', blk)])

        for ti in range(33):
            if ti < 32:
                pool_band(ti)
            if ti >= 1:
                pool_w_(ti - 1)
        B.P.barrier()

        outproj(1, w_out1b, False)
        ffn(1, True)

        if DEBUG:
            B.P.barrier()
        B.P.emit()
        B.es.close()
        return nc


_CACHE = {}


def _consts():
    if 'c' in _CACHE:
        return _CACHE['c']
    bf = ml_dtypes.bfloat16
    ident = np.eye(128, dtype=np.float32)
    d = np.arange(128)
    ang = 2 * np.pi * np.outer(d, d) / 128.0
    dftd = np.concatenate([np.cos(ang), -np.sin(ang)], axis=1) / np.sqrt(128.0)
    k = np.arange(128)[:, None]
    i = np.arange(128)[None, :]
    masks = np.stack([(k <= i), (k > i), (k >= i), (k < i)], axis=1).astype(np.float32)
    onesm = np.ones((128, 128), np.float32)
    pb = np.zeros((128, 4, 5, 128), np.float32)

    def blockT(w, jt, it):
        r = w // 2
        jj = (jt * 128 + np.arange(128))[:, None]
        ii = (it * 128 + np.arange(128))[None, :]
        cnt = np.minimum(ii + r + 1, L) - np.maximum(ii - r, 0)
        return (np.abs(ii - jj) <= r).astype(np.float64) / cnt - (ii == jj)

    for wi, w in enumerate((2, 4, 8, 16)):
        pb[:, wi, 0, :] = blockT(w, 5, 5)
        pb[:, wi, 1, :] = blockT(w, 4, 5)
        pb[:, wi, 2, :] = blockT(w, 6, 5)
        pb[:, wi, 3, :] = blockT(w, 0, 0)
        pb[:, wi, 4, :] = blockT(w, 31, 31)
    cb16 = np.concatenate([dftd, masks.reshape(128, 512), ident, onesm, pb.reshape(128, 20 * 128)], axis=1).astype(bf)
    p = np.arange(128)
    a = np.arange(32)
    l_idx = (a[None, :] * 128 + p[:, None])
    tcs = np.empty((32, 128, 2, 4096), dtype=bf)
    for kc in range(32):
        kk = kc * 128 + np.arange(128)
        prod = (l_idx[:, :, None] * kk[None, None, :]) % L
        angp = prod.astype(np.float64) * (2 * np.pi / L)
        tcs[kc, :, 0, :] = (np.cos(angp) / 64.0).reshape(128, 4096).astype(bf)
        tcs[kc, :, 1, :] = (np.sin(angp) / 64.0).reshape(128, 4096).astype(bf)
    _CACHE['c'] = dict(ident=ident, cb16=cb16, tcs=tcs.reshape(32, 128, 8192))
    return _CACHE['c']


def _pk(w, kc):
    n = w.shape[1]
    return np.ascontiguousarray(w.reshape(kc, 128, n).transpose(1, 0, 2).reshape(128, kc * n))


def _colv(v):
    return np.ascontiguousarray(v.reshape(-1, 128).T)


def kernel(x, c, ctx, c_ctx, ada_w, ada_b, mix_norm_g, ffn_norm_g, ffn_w1, ffn_w2,
           ev_w_in, ev_conv_w, ev_conv_b, ev_dt_bias, ev_a_log, ev_d_skip, ev_ssd_norm_g, ev_w_out,
           od_w_in, od_sgu_ln_g, od_sgu_ln_b, od_w_s, od_b_s, od_pool_w, od_pool_scale, od_w_out,
           final_g):
    f = lambda a: np.asarray(a, dtype=np.float32)
    x, c, ctx, c_ctx = f(x), f(c), f(ctx), f(c_ctx)
    cs = _consts()
    if 'nc' not in _CACHE:
        _CACHE['nc'] = Builder().build()
    nc = _CACHE['nc']
    shared = {
        "ada_w": np.stack([_pk(f(ada_w)[l], 8) for l in range(2)]),
        "w_in0": _pk(f(ev_w_in)[0], 8),
        "w_out0": _pk(f(ev_w_out)[0], 12),
        "w1": np.stack([_pk(f(ffn_w1)[l], 8) for l in range(2)]),
        "w2": np.stack([_pk(f(ffn_w2)[l], 32) for l in range(2)]),
        "w_in1": _pk(f(od_w_in)[0], 8),
        "w_out1": _pk(f(od_w_out)[0], 12),
        "w_sT": np.ascontiguousarray(f(od_w_s)[0].transpose(2, 0, 1).reshape(128, 512)),
        "pool_w": np.ascontiguousarray(f(od_pool_w)[0].transpose(1, 0, 2).reshape(128, 512)),
        "ident": cs['ident'], "cb16": cs['cb16'], "tcs": cs['tcs'],
    }
    rows = np.zeros((1, NROWS), np.float32)
    for name, v in [('dtbias', f(ev_dt_bias)[0].reshape(-1)), ('alog', f(ev_a_log)[0].reshape(-1)), ('dskip', f(ev_d_skip)[0]),
                    ('bs', f(od_b_s)[0].reshape(-1)), ('lng', f(od_sgu_ln_g)[0]), ('convb', f(ev_conv_b)[0])]:
        o, n = ROWS[name]
        rows[0, o:o + n] = v
    in_maps = []
    for b in range(8):
        cols = np.zeros((128, NCOLS), np.float32)

        def put(name, arr):
            o, n = COLS[name]
            cols[:, o:o + n] = arr
        put('gmix0', _colv(f(mix_norm_g)[0])); put('gmix1', _colv(f(mix_norm_g)[1]))
        put('gffn0', _colv(f(ffn_norm_g)[0])); put('gffn1', _colv(f(ffn_norm_g)[1]))
        put('adab0', _colv(f(ada_b)[0])); put('adab1', _colv(f(ada_b)[1]))
        cw = f(ev_conv_w)[0]
        put('convw', np.concatenate([_colv(cw[k]) for k in range(5)], axis=1))
        put('convb', _colv(f(ev_conv_b)[0]))
        put('ssdg', _colv(f(ev_ssd_norm_g)[0]))
        put('pscale', _colv(f(od_pool_scale)[0]))
        put('fing', _colv(f(final_g)))
        put('cb', _colv(c[b])); put('cctx', _colv(c_ctx))
        put('lnb', _colv(f(od_sgu_ln_b)[0]))
        m = dict(shared)
        m["x"] = np.ascontiguousarray(x[b])
        m["ctx"] = np.ascontiguousarray(ctx[b])
        m["cols"] = cols
        m["rows"] = rows
        in_maps.append(m)
    res = run_bass_kernel_spmd(nc, in_maps, core_ids=list(range(8)))
    return np.stack([np.asarray(r["out"], dtype=np.float32) for r in res.results], axis=0)
```
